# Optimizing a Trainium2 kernel written in Bass

```python
import jax, jax.numpy as jnp
from jax import lax
import numpy as np

D_MODEL = 2048
BATCH = 2
SEQ = 16384
DEPTH = 2

HEAD_DIM = 128
ATTN_GROUPS = ((128, 1), (512, 4), (2048, 16))
HEADS_PER_GROUP = 4
N_ATTN_HEADS = HEADS_PER_GROUP * len(ATTN_GROUPS)
ATTN_WIDTH = N_ATTN_HEADS * HEAD_DIM
ATTN_OUT_WIDTH = HEADS_PER_GROUP * HEAD_DIM
BAND_BLOCK = 128
ROPE_THETA = 10000.0
POOL_WINDOWS = (2, 4, 8, 16)
POOL_WIDTH = D_MODEL // 2
POOL_GROUP_WIDTH = POOL_WIDTH // len(POOL_WINDOWS)
N_BRANCHES = 2
IN_WIDTH = 3 * ATTN_WIDTH + POOL_WIDTH + N_BRANCHES * D_MODEL
PEER_HEADS = 8
PEER_N_KEYS = 128
PEER_N_EXPERTS = PEER_N_KEYS * PEER_N_KEYS
PEER_KEY_DIM = 256
PEER_HALF = PEER_KEY_DIM // 2
PEER_TOPK = 16
PEER_TOKEN_BLOCK = 128
LN_EPS = 1e-5
DEEPNORM_ALPHA = (2 * DEPTH) ** 0.25
DEEPNORM_BETA = (8 * DEPTH) ** -0.25

kernel_name = "hybrid_dilated_pool_peer_deepnorm"


def layer_norm(x, g, b):
    xf = x.astype(jnp.float32)
    mu = xf.mean(-1, keepdims=True)
    var = jnp.square(xf - mu).mean(-1, keepdims=True)
    return ((xf - mu) * lax.rsqrt(var + LN_EPS) * g.astype(jnp.float32) + b.astype(jnp.float32)).astype(x.dtype)


def rope_tables(seq_len):
    inv = ROPE_THETA ** (-jnp.arange(0, HEAD_DIM, 2, dtype=jnp.float32) / HEAD_DIM)
    ang = jnp.arange(seq_len, dtype=jnp.float32)[:, None] * inv[None, :]
    ang = jnp.concatenate([ang, ang], axis=-1)
    return jnp.cos(ang), jnp.sin(ang)


def apply_rope(t, cos, sin):
    t32 = t.astype(jnp.float32)
    t1, t2 = jnp.split(t32, 2, axis=-1)
    rot = jnp.concatenate([-t2, t1], axis=-1)
    return (t32 * cos[:, None, :] + rot * sin[:, None, :]).astype(t.dtype)


def dilated_window_attention(q, k, v, window, dilation):
    B, S, H, Dh = q.shape
    r = dilation
    n_back = window // dilation
    L = S // r
    nblk = -(-L // BAND_BLOCK)
    Lp = nblk * BAND_BLOCK

    def to_strided(t):
        t = t.reshape(B, L, r, H, Dh).transpose(0, 2, 1, 3, 4)
        return jnp.pad(t, ((0, 0), (0, 0), (0, Lp - L), (0, 0), (0, 0)))

    qs, ks, vs = to_strided(q), to_strided(k), to_strided(v)
    qb = qs.reshape(B, r, nblk, BAND_BLOCK, H, Dh)

    def band_keys(t):
        prev = jnp.pad(t, ((0, 0), (0, 0), (BAND_BLOCK, 0), (0, 0), (0, 0)))[:, :, :Lp]
        return jnp.concatenate([prev.reshape(B, r, nblk, BAND_BLOCK, H, Dh),
                                t.reshape(B, r, nblk, BAND_BLOCK, H, Dh)], axis=3)

    kb, vb = band_keys(ks), band_keys(vs)
    s = jnp.einsum('brnqhd,brnkhd->brnhqk', qb, kb, preferred_element_type=jnp.float32) * (Dh ** -0.5)
    qa = jnp.arange(BAND_BLOCK)[:, None]
    kbi = jnp.arange(2 * BAND_BLOCK)[None, :]
    dist = qa + BAND_BLOCK - kbi
    band = (dist >= 0) & (dist <= n_back)
    blk = jnp.arange(nblk)[:, None, None]
    mask = band[None] & ((blk > 0) | (kbi >= BAND_BLOCK)[None])
    s = jnp.where(mask[None, None, :, None], s, -jnp.inf)
    m = s.max(-1, keepdims=True)
    p = jnp.exp(s - m)
    den = p.sum(-1, keepdims=True)
    o = jnp.einsum('brnhqk,brnkhd->brnqhd', (p * (1.0 / den)).astype(v.dtype), vb)
    lse = (m + jnp.log(den))[..., 0]

    def from_strided(t):
        t = t[:, :, :L]
        return jnp.moveaxis(t, 1, 2).reshape((B, S) + t.shape[3:])

    o = from_strided(o.reshape(B, r, Lp, H, Dh))
    lse = from_strided(jnp.swapaxes(lse, -1, -2).reshape(B, r, Lp, H))
    return o, lse


def multiscale_pool(xp, w_pool_group, pool_scale):
    B, S, _ = xp.shape
    xf = xp.reshape(B, S, len(POOL_WINDOWS), POOL_GROUP_WIDTH).astype(jnp.float32)
    cs = jnp.cumsum(xf, axis=1)
    t = jnp.arange(S, dtype=jnp.float32)
    outs = []
    for g, w in enumerate(POOL_WINDOWS):
        c = cs[:, :, g]
        c_shift = jnp.pad(c, ((0, 0), (w, 0), (0, 0)))[:, :S]
        count = jnp.minimum(t + 1.0, float(w))
        outs.append((c - c_shift) / count[None, :, None] - xf[:, :, g])
    pooled = jnp.stack(outs, axis=2).astype(xp.dtype)
    mixed = jnp.einsum('bsgc,gcd->bsgd', pooled, w_pool_group)
    return mixed.reshape(B, S, POOL_WIDTH) * pool_scale


def hybrid_mixer(x, w_in, b_gate, w_branch_attn, w_branch_pool, w_pool_group, pool_scale, w_out, cos, sin):
    B, S, _ = x.shape
    z = x @ w_in
    q, k, v, xp, gl = jnp.split(z, [ATTN_WIDTH, 2 * ATTN_WIDTH, 3 * ATTN_WIDTH,
                                    3 * ATTN_WIDTH + POOL_WIDTH], axis=-1)
    q = apply_rope(q.reshape(B, S, N_ATTN_HEADS, HEAD_DIM), cos, sin)
    k = apply_rope(k.reshape(B, S, N_ATTN_HEADS, HEAD_DIM), cos, sin)
    v = v.reshape(B, S, N_ATTN_HEADS, HEAD_DIM)
    outs, lses = [], []
    for g, (window, dilation) in enumerate(ATTN_GROUPS):
        sl = slice(g * HEADS_PER_GROUP, (g + 1) * HEADS_PER_GROUP)
        o, l = dilated_window_attention(q[:, :, sl], k[:, :, sl], v[:, :, sl], window, dilation)
        outs.append(o)
        lses.append(l)
    wts = jax.nn.softmax(jnp.stack(lses, axis=0), axis=0)
    attn = jnp.einsum('gbsh,gbshd->bshd', wts.astype(x.dtype), jnp.stack(outs, axis=0))
    branch_a = attn.reshape(B, S, ATTN_OUT_WIDTH) @ w_branch_attn
    branch_b = multiscale_pool(xp, w_pool_group, pool_scale) @ w_branch_pool
    gates = jax.nn.sigmoid(gl.reshape(B, S, N_BRANCHES, D_MODEL) + b_gate)
    merged = gates[:, :, 0] * branch_a + gates[:, :, 1] * branch_b
    return merged @ w_out


def peer_ffn(x, w_peer_q, peer_subkeys, peer_u, peer_v):
    B, S, D = x.shape
    T = PEER_TOKEN_BLOCK
    xb = x.reshape(B * S // T, T, D)

    def block(xt):
        q = (xt @ w_peer_q).reshape(T, PEER_HEADS, 2, PEER_HALF)
        s = jnp.einsum('thcd,hckd->thck', q, peer_subkeys, preferred_element_type=jnp.float32)
        sv, si = lax.top_k(s, PEER_TOPK)
        cand = sv[:, :, 0, :, None] + sv[:, :, 1, None, :]
        cv, ci = lax.top_k(cand.reshape(T, PEER_HEADS, PEER_TOPK * PEER_TOPK), PEER_TOPK)
        i1 = jnp.take_along_axis(si[:, :, 0], ci // PEER_TOPK, axis=-1)
        i2 = jnp.take_along_axis(si[:, :, 1], ci % PEER_TOPK, axis=-1)
        experts = i1 * PEER_N_KEYS + i2
        gate = jax.nn.softmax(cv, axis=-1).astype(xt.dtype)
        h = jnp.einsum('thkd,td->thk', peer_u[experts], xt)
        act = gate * jax.nn.gelu(h, approximate=False)
        return jnp.einsum('thk,thkd->td', act, peer_v[experts])

    return lax.map(block, xb).reshape(B, S, D)


def setup_inputs(seed: int = 0) -> dict:
    key = jax.random.key(seed)
    ks = jax.random.split(key, 17)
    L = DEPTH
    nrm = jax.random.normal
    f32 = jnp.float32
    return {
        "x": nrm(ks[0], (BATCH, SEQ, D_MODEL), f32),
        "w_in": nrm(ks[1], (L, D_MODEL, IN_WIDTH), f32) * D_MODEL ** -0.5,
        "b_gate": nrm(ks[2], (L, N_BRANCHES, D_MODEL), f32) * 0.02,
        "w_branch_attn": nrm(ks[3], (L, ATTN_OUT_WIDTH, D_MODEL), f32) * ATTN_OUT_WIDTH ** -0.5,
        "w_branch_pool": nrm(ks[4], (L, POOL_WIDTH, D_MODEL), f32) * POOL_WIDTH ** -0.5,
        "w_pool_group": nrm(ks[5], (L, len(POOL_WINDOWS), POOL_GROUP_WIDTH, POOL_GROUP_WIDTH), f32) * POOL_GROUP_WIDTH ** -0.5,
        "pool_scale": 1.0 + 0.02 * nrm(ks[6], (L, POOL_WIDTH), f32),
        "w_out": nrm(ks[7], (L, D_MODEL, D_MODEL), f32) * (D_MODEL ** -0.5 * DEEPNORM_BETA),
        "ln1_g": 1.0 + 0.02 * nrm(ks[8], (L, D_MODEL), f32),
        "ln1_b": 0.02 * nrm(ks[9], (L, D_MODEL), f32),
        "w_peer_q": nrm(ks[10], (L, D_MODEL, PEER_HEADS * PEER_KEY_DIM), f32) * D_MODEL ** -0.5,
        "peer_subkeys": nrm(ks[11], (L, PEER_HEADS, 2, PEER_N_KEYS, PEER_HALF), f32) * PEER_HALF ** -0.5,
        "peer_u": nrm(ks[12], (L, PEER_N_EXPERTS, D_MODEL), f32) * D_MODEL ** -0.5,
        "peer_v": nrm(ks[13], (L, PEER_N_EXPERTS, D_MODEL), f32) * (DEEPNORM_BETA * PEER_HEADS ** -0.5),
        "ln2_g": 1.0 + 0.02 * nrm(ks[14], (L, D_MODEL), f32),
        "ln2_b": 0.02 * nrm(ks[15], (L, D_MODEL), f32),
    }


def reference(x, w_in, b_gate, w_branch_attn, w_branch_pool, w_pool_group, pool_scale, w_out,
              ln1_g, ln1_b, w_peer_q, peer_subkeys, peer_u, peer_v, ln2_g, ln2_b):
    cos, sin = rope_tables(x.shape[1])
    for l in range(DEPTH):
        y = hybrid_mixer(x, w_in[l], b_gate[l], w_branch_attn[l], w_branch_pool[l],
                         w_pool_group[l], pool_scale[l], w_out[l], cos, sin)
        x = layer_norm(DEEPNORM_ALPHA * x + y, ln1_g[l], ln1_b[l])
        y = peer_ffn(x, w_peer_q[l], peer_subkeys[l], peer_u[l], peer_v[l])
        x = layer_norm(DEEPNORM_ALPHA * x + y, ln2_g[l], ln2_b[l])
    return x
```

```python
from contextlib import ExitStack
from concourse.bass_utils import run_bass_kernel_spmd
import numpy as np
import concourse.bass as bass
import concourse.mybir as mybir

F32 = mybir.dt.float32
BF16 = mybir.dt.bfloat16
I32 = mybir.dt.int32
U32 = mybir.dt.uint32
AF = mybir.ActivationFunctionType
ALU = mybir.AluOpType

COMPUTE = ("pe", "act", "dve", "pool")
ENGINES = ("pe", "act", "dve", "pool", "sp")


class Res:
    __slots__ = ("name", "w", "r", "excl")

    def __init__(self, name, excl=False):
        self.name = name
        self.excl = excl
        self.w = None
        self.r = []


class Op:
    __slots__ = ("eng", "fn", "deps", "is_dma", "seq", "sem", "has_dep", "idx", "group")

    def __init__(self, eng, fn, is_dma):
        self.eng = eng
        self.fn = fn
        self.deps = []
        self.is_dma = is_dma
        self.seq = None
        self.sem = None
        self.has_dep = False
        self.group = None


class Sched:
    NDMA_SEM = 6

    def __init__(self, nc, es, same_engine_sync=True):
        self.nc = nc
        self.ops = []
        self.same_engine_sync = same_engine_sync
        self.sem = {e: es.enter_context(nc.semaphore("c_" + e)) for e in ENGINES}
        self.cnt = {e: 0 for e in ENGINES}
        self.dsem = {e: [es.enter_context(nc.semaphore(f"d_{e}{i}")) for i in range(self.NDMA_SEM)]
                     for e in ("sp", "act", "pool", "pe", "dve")}
        self.dcnt = {e: [0] * self.NDMA_SEM for e in self.dsem}
        self.drr = {e: 0 for e in self.dsem}
        self.dlast = {e: [None] * self.NDMA_SEM for e in self.dsem}
        self.waited = {}
        self.pending_dma = []
        self.nres = 0

    def res(self, name=None, excl=False):
        self.nres += 1
        return Res(name or f"r{self.nres}", excl)

    def _track(self, op, r, w):
        deps = set()
        g = op.group
        xr = [x for x in r if x.excl and x not in w]
        r = [x for x in r if not (x.excl and x not in w)]
        w = list(w) + xr
        for x in r:
            for o in (x.w or ()):
                deps.add(o)
        for x in w:
            for o in (x.w or ()):
                if g is None or o.group != g:
                    deps.add(o)
            for o in x.r:
                deps.add(o)
        deps.discard(op)
        op.deps = list(deps)
        for x in r:
            x.r.append(op)
        for x in w:
            if g is not None and x.w and x.w[-1].group == g and not x.r:
                x.w.append(op)
            else:
                x.w = [op]
            x.r = []

    def op(self, eng, fn, r=(), w=(), group=None):
        o = Op(eng, fn, False)
        o.group = group
        self._track(o, r, w)
        self.ops.append(o)
        return o

    def dma(self, eng, out, in_, r=(), w=(), group=None, **kw):
        if eng == "pool":
            kw = dict(kw)
            kw.setdefault("max_dma_last_dim", 4096)

        def fn(e, out=out, in_=in_, kw=kw):
            return e.dma_start(out=out, in_=in_, **kw)
        o = Op(eng, fn, True)
        o.group = group
        self._track(o, r, w)
        self.ops.append(o)
        return o

    def _eng(self, name):
        nc = self.nc
        return {"pe": nc.tensor, "act": nc.scalar, "dve": nc.vector, "pool": nc.gpsimd, "sp": nc.sync}[name]

    def flush(self, final_wait=False):
        nc = self.nc
        ops = self.ops
        self.ops = []
        if not ops:
            return
        for o in ops:
            for d in o.deps:
                d.has_dep = True
        per = {e: [] for e in ENGINES}
        for o in ops:
            waits = []
            e = o.eng
            if o.is_dma:
                k = self.drr[e]
                self.drr[e] = (k + 1) % (3 if e == "pool" else self.NDMA_SEM)
                prev = self.dlast[e][k]
                if prev is not None:
                    waits.append((prev.sem, prev.seq))
                self.dcnt[e][k] += 16
                o.sem = self.dsem[e][k]
                o.seq = self.dcnt[e][k]
                self.dlast[e][k] = o
            else:
                if o.has_dep or True:
                    self.cnt[e] += 1
                    o.sem = self.sem[e]
                    o.seq = self.cnt[e]
            for d in o.deps:
                if d.seq is None:
                    raise RuntimeError("dep not scheduled")
                if (not d.is_dma) and d.eng == e:
                    if e == "pe" or not self.same_engine_sync:
                        continue
                waits.append((d.sem, d.seq))
            mx = {}
            for s, v in waits:
                key = id(s)
                if key not in mx or mx[key][1] < v:
                    mx[key] = (s, v)
            fin = []
            for key, (s, v) in mx.items():
                wk = (e, key)
                if self.waited.get(wk, 0) >= v:
                    continue
                self.waited[wk] = v
                fin.append((s, v))
            per[e].append((o, fin))

        if final_wait:
            self.final_waits = [(self.dsem[e][k], self.dcnt[e][k]) for e in self.dsem
                                for k in range(self.NDMA_SEM) if self.dcnt[e][k] > 0]
        else:
            self.final_waits = []

        with nc.Block() as block:
            def mk(ename):
                lst = per[ename]
                fw = self.final_waits if ename == "sp" else []

                def body(eng):
                    for o, waits in lst:
                        for s, v in waits:
                            eng.wait_ge(s, v)
                        ins = o.fn(eng)
                        if o.is_dma:
                            ins.then_inc(o.sem, 16)
                        else:
                            ins.then_inc(o.sem, 1)
                    for s, v in fw:
                        eng.wait_ge(s, v)
                return body
            for ename, reg in (("pe", block.tensor), ("act", block.scalar), ("dve", block.vector),
                               ("pool", block.gpsimd), ("sp", block.sync)):
                if per[ename] or (ename == "sp" and self.final_waits):
                    reg(mk(ename))
        if final_wait:
            self.pending_dma = []
AX = mybir.AxisListType
ALPHA = float(4 ** 0.25)
NEG = -1.0e30


_UID = [0]


class Ring:
    def __init__(self, S, es, nc, name, shape, dtype, n, psum=False):
        self.t = []
        _UID[0] += 1
        name = f"{name}u{_UID[0]}"
        for i in range(n):
            mk = nc.psum_tensor if psum else nc.sbuf_tensor
            self.t.append((es.enter_context(mk(f"{name}_{i}", shape, dtype)), S.res(f"{name}{i}", excl=psum)))
        self.i = 0

    def next(self):
        x = self.t[self.i % len(self.t)]
        self.i += 1
        return x


def sbt(S, es, nc, name, shape, dtype):
    _UID[0] += 1
    name = f"{name}u{_UID[0]}"
    return es.enter_context(nc.sbuf_tensor(name, shape, dtype)), S.res(name)


_GRP = [0]


def cast_load(S, dst, src, rdst):
    K_, N_ = dst.shape[1], dst.shape[2]
    _GRP[0] += 1
    if N_ <= 1024:
        S.dma("pool", dst, src, w=[rdst])
    else:
        for k in range(K_):
            S.dma("pool", dst[:, k, :], src[:, k, :], w=[rdst], group=("cl", _GRP[0]))


def load_consts(nc, S, es, T):
    C = {}
    C["ident"], C["r_ident"] = sbt(S, es, nc, "c_ident", [128, 128], F32)
    C["iota"], C["r_iota"] = sbt(S, es, nc, "c_iota", [128, 128], F32)
    C["pswap"], C["r_pswap"] = sbt(S, es, nc, "c_pswap", [128, 128], BF16)
    C["onesb"], C["r_onesb"] = sbt(S, es, nc, "c_onesb", [128, 128], BF16)
    C["onesd"], C["r_onesd"] = sbt(S, es, nc, "c_onesd", [128, 128], BF16)
    C["masks"], C["r_masks"] = sbt(S, es, nc, "c_masks", [128, 2, 256], F32)
    C["flag"], C["r_flag"] = sbt(S, es, nc, "c_flag", [128, 1], F32)
    C["corr"], C["r_corr"] = sbt(S, es, nc, "c_corr", [128, 4, 16], F32)
    S.dma("sp", C["ident"][:], T["ident"], w=[C["r_ident"]])
    S.dma("sp", C["iota"][:], T["iota"], w=[C["r_iota"]])
    S.dma("pool", C["pswap"][:], T["pswap"], w=[C["r_pswap"]])
    S.dma("sp", C["masks"][:], T["masks"], w=[C["r_masks"]])
    S.dma("sp", C["flag"][:], T["flag"], w=[C["r_flag"]])
    S.dma("sp", C["corr"][:], T["corr"], w=[C["r_corr"]])
    C["masksb"], C["r_masksb"] = sbt(S, es, nc, "c_masksb", [128, 2, 256], BF16)
    S.op("dve", lambda e: e.tensor_copy(C["masksb"][:], C["masks"][:]), r=[C["r_masks"]], w=[C["r_masksb"]])
    S.op("dve", lambda e: e.memset(C["onesb"][:], 1.0), w=[C["r_onesb"]])
    S.op("dve", lambda e: e.memset(C["onesd"][:], 1.0 / 2048.0), w=[C["r_onesd"]])
    C["one"], C["r_one"] = sbt(S, es, nc, "c_one", [128, 1], F32)
    S.op("dve", lambda e: e.memset(C["one"][:], 1.0), w=[C["r_one"]])
    return C


def phase_A(nc, S, C, T, W, cfg):
    with ExitStack() as es:
        xsb, rx = sbt(S, es, nc, "A_x", [128, 16, 2048], BF16)
        cs, rcs = sbt(S, es, nc, "A_cs", [128, 2, 2048], F32)
        bg, rbg = sbt(S, es, nc, "A_bg", [128, 32], F32)
        wring = Ring(S, es, nc, "A_w", [128, 16, 512], BF16, 2)
        pring = Ring(S, es, nc, "A_ps", [128, 512], F32, 4, psum=True)
        p2ring = Ring(S, es, nc, "A_ps2", [128, 512], F32, 2, psum=True)
        zring = Ring(S, es, nc, "A_zb", [128, 512], BF16, 2)
        t1ring = Ring(S, es, nc, "A_t1", [128, 512], F32, 2)
        t2ring = Ring(S, es, nc, "A_t2", [128, 512], F32, 2)
        string = Ring(S, es, nc, "A_st", [128, 512], BF16, 4)
        sfring = Ring(S, es, nc, "A_sf", [128, 512], F32, 2)
        S.dma("sp", bg[:], W["b_gate"], w=[rbg])
        flip = [0]
        for sg in range(cfg.NALL // 2048):
            cast_load(S, xsb[:], T["xT"].rearrange("(k p) t -> p k t", p=128)[:, :, sg * 2048:(sg + 1) * 2048], rx)
            S.dma("sp", cs[:, 0, :], T["cosT"][:, cfg.coff + sg * 2048:cfg.coff + (sg + 1) * 2048], w=[rcs])
            S.dma("sp", cs[:, 1, :], T["sinT"][:, cfg.coff + sg * 2048:cfg.coff + (sg + 1) * 2048], w=[rcs])
            blocks = [3, 4, 5, 6, 7, 8, 9, 10] if sg == 0 else list(range(19))
            for blk in blocks:
                wb, rw = wring.next()
                S.dma("pool", wb[:], W["w_in"].rearrange("(k p) c -> p k c", p=128)[:, :, blk * 512:(blk + 1) * 512], w=[rw])
                subs = [0, 1, 2, 3] if (sg > 0 or blk in (5, 8)) else [3]
                for sub in subs:
                    j0 = sg * 2048 + sub * 512
                    t0 = j0 - 2048
                    if 6 <= blk <= 8:
                        for tt in range(4):
                            ps, rps = pring.next()
                            for k in range(16):
                                S.op("pe", lambda e, ps=ps, k=k, wb=wb, c0=sub * 512 + tt * 128: e.matmul(
                                    ps[:], xsb[:, k, c0:c0 + 128], wb[:, k, :], start=(k == 0), stop=(k == 15)),
                                    r=[rx, rw], w=[rps])
                            st, rst = string.next()
                            flip[0] ^= 1
                            if flip[0]:
                                S.op("act", lambda e, st=st, ps=ps: e.activation(st[:], ps[:], AF.Copy), r=[rps], w=[rst])
                            else:
                                S.op("dve", lambda e, st=st, ps=ps: e.tensor_copy(st[:], ps[:]), r=[rps], w=[rst])
                            S.dma("sp", T["vS"][j0 + tt * 128:j0 + tt * 128 + 128, (blk - 6) * 512:(blk - 5) * 512], st[:], r=[rst])
                        continue
                    for mc in range(4):
                        ps, rps = pring.next()
                        for k in range(16):
                            S.op("pe", lambda e, ps=ps, k=k, wb=wb, mc=mc, sub=sub: e.matmul(
                                ps[:], wb[:, k, mc * 128:(mc + 1) * 128], xsb[:, k, sub * 512:(sub + 1) * 512],
                                start=(k == 0), stop=(k == 15)), r=[rx, rw], w=[rps])
                        if blk < 6:
                            zb, rzb = zring.next()
                            ps2, rps2 = p2ring.next()
                            t1, rt1 = t1ring.next()
                            t2, rt2 = t2ring.next()
                            st, rst = string.next()
                            S.op("act", lambda e, zb=zb, ps=ps: e.activation(zb[:], ps[:], AF.Copy), r=[rps], w=[rzb])
                            S.op("pe", lambda e, ps2=ps2, zb=zb: e.matmul(ps2[:], C["pswap"][:], zb[:], start=True, stop=True),
                                 r=[rzb, C["r_pswap"]], w=[rps2])
                            S.op("dve", lambda e, t1=t1, ps=ps, sub=sub: e.tensor_tensor(
                                t1[:], ps[:], cs[:, 0, sub * 512:(sub + 1) * 512], ALU.mult), r=[rps, rcs], w=[rt1])
                            S.op("dve", lambda e, t2=t2, ps2=ps2, sub=sub: e.tensor_tensor(
                                t2[:], ps2[:], cs[:, 1, sub * 512:(sub + 1) * 512], ALU.mult), r=[rps2, rcs], w=[rt2])
                            S.op("pool", lambda e, st=st, t1=t1, t2=t2: e.tensor_tensor(st[:], t1[:], t2[:], ALU.add),
                                 r=[rt1, rt2], w=[rst])
                            if blk < 3:
                                row = (blk * 4 + mc) * 128
                                S.dma("sp", T["qS"][row:row + 128, t0:t0 + 512], st[:], r=[rst])
                            else:
                                row = ((blk - 3) * 4 + mc) * 128
                                S.dma("sp", T["kS"][row:row + 128, j0:j0 + 512], st[:], r=[rst])
                        elif blk < 11:
                            sf, rsf = sfring.next()
                            row = ((blk - 9) * 4 + mc) * 128
                            if sg == 0:
                                S.op("act", lambda e, sf=sf, ps=ps: e.activation(sf[:, 0:16], ps[:, 496:512], AF.Copy, scale=C["flag"][:, 0:1]),
                                     r=[rps, C["r_flag"]], w=[rsf])
                                S.dma("sp", T["xpS"][row:row + 128, 0:16], sf[:, 0:16], r=[rsf])
                            else:
                                S.op("act", lambda e, sf=sf, ps=ps: e.activation(sf[:], ps[:], AF.Copy), r=[rps], w=[rsf])
                                S.dma("sp", T["xpS"][row:row + 128, 16 + t0:16 + t0 + 512], sf[:], r=[rsf])
                        else:
                            c = (blk - 11) * 4 + mc
                            st, rst = string.next()
                            S.op("act", lambda e, st=st, ps=ps, c=c: e.activation(st[:], ps[:], AF.Sigmoid, bias=bg[:, c:c + 1]),
                                 r=[rps, rbg], w=[rst])
                            S.dma("sp", T["gS"][c * 128:(c + 1) * 128, t0:t0 + 512], st[:], r=[rst])
        S.flush(final_wait=True)


def phase_B(nc, S, C, T, cfg):
    NOWN, NALL = cfg.NOWN, cfg.NALL
    scale = float(128 ** -0.5)
    with ExitStack() as es:
        kring = Ring(S, es, nc, "B_k", [128, NALL], BF16, 2)
        qring = Ring(S, es, nc, "B_q", [128, NOWN], BF16, 2)
        vring = Ring(S, es, nc, "B_v", [128, NALL // 128, 128], BF16, 2)
        ndring = Ring(S, es, nc, "B_nd", [128, 2, NOWN], F32, 1)
        aring = Ring(S, es, nc, "B_at", [128, NOWN], BF16, 1)
        psS = Ring(S, es, nc, "B_pss", [128, 512], F32, 3, psum=True)
        psO = Ring(S, es, nc, "B_pso", [128, 512], F32, 3, psum=True)
        pT1 = Ring(S, es, nc, "B_p1", [128, 256], BF16, 3)
        pT2 = Ring(S, es, nc, "B_p2", [128, 256], BF16, 3)
        vrows = T["vS"]
        for h in range(4):
            nd, rnd = ndring.next()
            for g, r in enumerate((1, 4, 16)):
                hh = g * 4 + h
                kt, rk = kring.next()
                qt, rq = qring.next()
                vt, rv = vring.next()
                S.dma("sp", kt[:], T["kS"][hh * 128:(hh + 1) * 128, :], w=[rk])
                S.dma("sp", qt[:], T["qS"][hh * 128:(hh + 1) * 128, :], w=[rq])
                nkb = (NALL // 128) // r
                for p in range(r):
                    src = bass.AP(vrows.tensor, vrows.offset + p * 1536 + hh * 128,
                                  [[r * 1536, 128], [128 * r * 1536, nkb], [1, 128]])
                    S.dma("sp", vt[:, p * nkb:(p + 1) * nkb, :], src, w=[rv])
                nqb = (NOWN // 128) // r
                off = 16 // r
                for p in range(r):
                    for qb in range(nqb):
                        kb = qb + off
                        ps, rps = psS.next()
                        po, rpo = psO.next()
                        p1, rp1 = pT1.next()
                        p2, rp2 = pT2.next()
                        qs = (qb * 128) * r + p
                        qap = qt[:, qs:qs + 127 * r + 1:r]
                        for i, kk in enumerate((kb - 1, kb)):
                            ks = (kk * 128) * r + p
                            S.op("pe", lambda e, ps=ps, i=i, ks=ks, kt=kt, qap=qap, r=r: e.matmul(
                                ps[:, i * 128:(i + 1) * 128], kt[:, ks:ks + 127 * r + 1:r], qap, start=True, stop=True),
                                r=[rk, rq], w=[rps])
                        S.op("act", lambda e, p1=p1, ps=ps: e.activation(p1[:], ps[:, 0:256], AF.Exp, scale=scale), r=[rps], w=[rp1])
                        mi = 1 if qb == cfg.mask_qb(r) else 0
                        S.op("pool" if (qb + p) % 2 == 0 else "dve", lambda e, p2=p2, p1=p1, mi=mi: e.tensor_tensor(
                            p2[:], p1[:], C["masksb"][:, mi, :], ALU.mult), r=[rp1, C["r_masksb"]], w=[rp2])
                        for i, kk in enumerate((kb - 1, kb)):
                            S.op("pe", lambda e, po=po, i=i, kk=kk, vt=vt, p2=p2, p=p, nkb=nkb: e.matmul(
                                po[:, 0:128], vt[:, p * nkb + kk, :], p2[:, i * 128:(i + 1) * 128], start=(i == 0), stop=(i == 1)),
                                r=[rv, rp2], w=[rpo])
                        for i in range(2):
                            S.op("pe", lambda e, po=po, i=i, p2=p2: e.matmul(
                                po[:, 128:256], C["onesb"][:], p2[:, i * 128:(i + 1) * 128], start=(i == 0), stop=(i == 1)),
                                r=[C["r_onesb"], rp2], w=[rpo])
                        ndv = bass.AP(nd, qs, [[2 * NOWN, 128], [NOWN, 2], [r, 128]])
                        pov = bass.AP(po, 0, [[512, 128], [128, 2], [1, 128]])
                        if g == 0:
                            S.op("dve", lambda e, ndv=ndv, pov=pov: e.tensor_copy(ndv, pov), r=[rpo], w=[rnd], group=("nd", h))
                        else:
                            S.op("dve", lambda e, ndv=ndv, pov=pov: e.tensor_tensor(ndv, pov, ndv, ALU.add), r=[rpo], w=[rnd], group=("nd", h))
            at, rat = aring.next()
            S.op("dve", lambda e, nd=nd: e.reciprocal(nd[:, 1, :], nd[:, 1, :]), r=[rnd], w=[rnd])
            S.op("dve", lambda e, nd=nd, at=at: e.tensor_tensor(at[:], nd[:, 0, :], nd[:, 1, :], ALU.mult), r=[rnd], w=[rat])
            S.dma("sp", T["attnS"][h * 128:(h + 1) * 128, :], at[:], r=[rat])
        S.flush(final_wait=True)


def phase_C1(nc, S, C, T, W, cfg):
    NOWN = cfg.NOWN
    with ExitStack() as es:
        wbp, rwbp = sbt(S, es, nc, "C_wbp", [128, 8, 2048], BF16)
        wba, rwba = sbt(S, es, nc, "C_wba", [128, 4, 2048], BF16)
        wpg, rwpg = sbt(S, es, nc, "C_wpg", [128, 4, 2, 256], BF16)
        psc, rpsc = sbt(S, es, nc, "C_psc", [128, 8], F32)
        cast_load(S, wbp[:], W["w_branch_pool"].rearrange("(k p) n -> p k n", p=128), rwbp)
        cast_load(S, wba[:], W["w_branch_attn"].rearrange("(k p) n -> p k n", p=128), rwba)
        for g_ in range(4):
            S.dma("pool", wpg[:, g_, :, :], W["w_pool_group"][g_].rearrange("(k p) d -> p k d", p=128), w=[rwpg])
        S.dma("sp", psc[:], W["pool_scale"], w=[rpsc])
        xpr = Ring(S, es, nc, "C_xp", [128, 8, 528], F32, 1)
        tA = Ring(S, es, nc, "C_tA", [128, 2, 528], F32, 2)
        tB = Ring(S, es, nc, "C_tB", [128, 2, 528], F32, 2)
        poolr = Ring(S, es, nc, "C_pl", [128, 8, 512], BF16, 1)
        mixr = Ring(S, es, nc, "C_mx", [128, 8, 512], BF16, 1)
        atr = Ring(S, es, nc, "C_at", [128, 4, 512], BF16, 2)
        gr = Ring(S, es, nc, "C_g", [128, 2, 512], BF16, 4)
        mer = Ring(S, es, nc, "C_me", [128, 16, 512], BF16, 2)
        tu = Ring(S, es, nc, "C_tu", [128, 2, 512], F32, 3)
        pm = Ring(S, es, nc, "C_pm", [128, 512], F32, 2, psum=True)
        pa = Ring(S, es, nc, "C_pa", [128, 512], F32, 2, psum=True)
        pb = Ring(S, es, nc, "C_pb", [128, 512], F32, 2, psum=True)
        for tg in range(NOWN // 512):
            t0 = tg * 512
            xp, rxp = xpr.next()
            S.dma("sp", xp[:], T["xpS"].rearrange("(c p) t -> p c t", p=128)[:, :, t0:t0 + 528], w=[rxp])
            at, rat = atr.next()
            S.dma("sp", at[:], T["attnS"].rearrange("(c p) t -> p c t", p=128)[:, :, t0:t0 + 512], w=[rat])
            pl, rpl = poolr.next()
            mx, rmx = mixr.next()
            me, rme = mer.next()
            for gi in range(4):
                w = 2 << gi
                cur = xp[:, 2 * gi:2 * gi + 2, :]
                rcur = rxp
                lo = 0
                sh = 1
                eng = ("dve", "pool")
                for step in range(gi + 1):
                    dst, rdst = (tA if step % 2 == 0 else tB).next()
                    n = 528 - (lo + sh)
                    S.op(eng[(gi + step) % 2], lambda e, dst=dst, cur=cur, lo=lo, sh=sh, n=n: e.tensor_tensor(
                        dst[:, :, lo + sh:lo + sh + n], cur[:, :, lo + sh:lo + sh + n], cur[:, :, lo:lo + n], ALU.add),
                        r=[rcur], w=[rdst])
                    cur, rcur = dst, rdst
                    lo += sh
                    sh *= 2
                if tg == cfg.corr_tg:
                    S.op("dve", lambda e, cur=cur, gi=gi: e.tensor_tensor(
                        cur[:, :, 16:32], cur[:, :, 16:32], C["corr"][:, gi:gi + 1, :].to_broadcast([128, 2, 16]), ALU.mult),
                        r=[rcur, C["r_corr"]], w=[rcur])
                S.op("dve", lambda e, pl=pl, cur=cur, gi=gi, w=w, xp=xp: e.scalar_tensor_tensor(
                    pl[:, 2 * gi:2 * gi + 2, :], cur[:, :, 16:528], 1.0 / w, xp[:, 2 * gi:2 * gi + 2, 16:528], ALU.mult, ALU.subtract),
                    r=[rcur, rxp], w=[rpl])
                for mo in range(2):
                    ps, rps = pm.next()
                    for ki in range(2):
                        S.op("pe", lambda e, ps=ps, gi=gi, ki=ki, mo=mo, pl=pl: e.matmul(
                            ps[:], wpg[:, gi, ki, mo * 128:(mo + 1) * 128], pl[:, 2 * gi + ki, :], start=(ki == 0), stop=(ki == 1)),
                            r=[rwpg, rpl], w=[rps])
                    S.op("act", lambda e, mx=mx, ps=ps, c=2 * gi + mo: e.activation(mx[:, c, :], ps[:], AF.Copy, scale=psc[:, c:c + 1]),
                         r=[rps, rpsc], w=[rmx])
            for m in range(16):
                g2, rg2 = gr.next()
                src = bass.AP(T["gS"].tensor, T["gS"].offset + m * 128 * NOWN + t0, [[NOWN, 128], [2048 * NOWN, 2], [1, 512]])
                S.dma("sp", g2[:], src, w=[rg2])
                psa, rpsa = pa.next()
                psb, rpsb = pb.next()
                for k in range(4):
                    S.op("pe", lambda e, psa=psa, k=k, m=m, at=at: e.matmul(
                        psa[:], wba[:, k, m * 128:(m + 1) * 128], at[:, k, :], start=(k == 0), stop=(k == 3)), r=[rwba, rat], w=[rpsa])
                for k in range(8):
                    S.op("pe", lambda e, psb=psb, k=k, m=m, mx=mx: e.matmul(
                        psb[:], wbp[:, k, m * 128:(m + 1) * 128], mx[:, k, :], start=(k == 0), stop=(k == 7)), r=[rwbp, rmx], w=[rpsb])
                t, rt = tu.next()
                S.op("dve", lambda e, t=t, psa=psa, g2=g2: e.tensor_tensor(t[:, 0, :], psa[:], g2[:, 0, :], ALU.mult), r=[rpsa, rg2], w=[rt])
                S.op("dve", lambda e, t=t, psb=psb, g2=g2: e.tensor_tensor(t[:, 1, :], psb[:], g2[:, 1, :], ALU.mult), r=[rpsb, rg2, rt], w=[rt])
                S.op("pool", lambda e, t=t, me=me, m=m: e.tensor_tensor(me[:, m, :], t[:, 0, :], t[:, 1, :], ALU.add), r=[rt], w=[rme])
            S.dma("sp", T["mergedS"].rearrange("(c p) t -> p c t", p=128)[:, :, t0:t0 + 512], me[:], r=[rme])
        S.flush(final_wait=True)


def layer_norm_T(nc, S, C, h, rh, ncols, gam, bet, rgb, rings, out_writer):
    hb_r, hs_r, ps_mean, ps_msq, st_r, tmp_r = rings
    pmean, rpm = ps_mean.next()
    pmsq, rpq = ps_msq.next()
    for m in range(16):
        hb, rhb = hb_r.next()
        hs, rhs = hs_r.next()
        S.op("dve", lambda e, hb=hb, m=m: e.tensor_copy(hb[:, :ncols], h[:, m, :]), r=[rh], w=[rhb])
        S.op("act", lambda e, hs=hs, m=m: e.activation(hs[:, :ncols], h[:, m, :], AF.Square), r=[rh], w=[rhs])
        S.op("pe", lambda e, hb=hb, m=m: e.matmul(pmean[:, :ncols], C["onesd"][:], hb[:, :ncols], start=(m == 0), stop=(m == 15)),
             r=[rhb, C["r_onesd"]], w=[rpm])
        S.op("pe", lambda e, hs=hs, m=m: e.matmul(pmsq[:, :ncols], C["onesd"][:], hs[:, :ncols], start=(m == 0), stop=(m == 15)),
             r=[rhs, C["r_onesd"]], w=[rpq])
    st, rst = st_r.next()
    S.op("dve", lambda e: e.tensor_copy(st[:, 0, :ncols], pmean[:, :ncols]), r=[rpm], w=[rst])
    S.op("dve", lambda e: e.tensor_tensor(st[:, 2, :ncols], st[:, 0, :ncols], st[:, 0, :ncols], ALU.mult), r=[rst], w=[rst])
    S.op("dve", lambda e: e.tensor_tensor(st[:, 2, :ncols], pmsq[:, :ncols], st[:, 2, :ncols], ALU.subtract), r=[rst, rpq], w=[rst])
    S.op("dve", lambda e: e.tensor_scalar(st[:, 2, :ncols], st[:, 2, :ncols], 1e-5, None, ALU.add), r=[rst], w=[rst])
    S.op("act", lambda e: e.activation(st[:, 2, :ncols], st[:, 2, :ncols], AF.Sqrt), r=[rst], w=[rst])
    S.op("dve", lambda e: e.reciprocal(st[:, 1, :ncols], st[:, 2, :ncols]), r=[rst], w=[rst])
    for m in range(16):
        tp, rtp = tmp_r.next()
        S.op("dve", lambda e, tp=tp, m=m: e.tensor_tensor(tp[:, :ncols], h[:, m, :], st[:, 0, :ncols], ALU.subtract), r=[rh, rst], w=[rtp])
        S.op("pool", lambda e, tp=tp: e.tensor_tensor(tp[:, :ncols], tp[:, :ncols], st[:, 1, :ncols], ALU.mult), r=[rtp, rst], w=[rtp])
        S.op("act", lambda e, tp=tp, m=m: e.activation(h[:, m, :], tp[:, :ncols], AF.Identity, bias=bet[:, m:m + 1], scale=gam[:, m:m + 1]),
             r=[rtp, rgb], w=[rh])
        out_writer(m)


def phase_C2(nc, S, C, T, W, cfg):
    with ExitStack() as es:
        wo, rwo = sbt(S, es, nc, "D_wo", [128, 16, 2048], BF16)
        gb, rgb = sbt(S, es, nc, "D_gb", [128, 2, 16], F32)
        cast_load(S, wo[:], W["w_out"].rearrange("(k p) n -> p k n", p=128), rwo)
        S.dma("sp", gb[:, 0, :], W["ln1_g"], w=[rgb])
        S.dma("sp", gb[:, 1, :], W["ln1_b"], w=[rgb])
        mer = Ring(S, es, nc, "D_me", [128, 16, 512], BF16, 2)
        hr = Ring(S, es, nc, "D_h", [128, 16, 512], F32, 1)
        xbr = Ring(S, es, nc, "D_xb", [128, 16, 512], BF16, 1)
        pp = Ring(S, es, nc, "D_pp", [128, 512], F32, 3, psum=True)
        rings = (Ring(S, es, nc, "D_hb", [128, 512], BF16, 3), Ring(S, es, nc, "D_hs", [128, 512], BF16, 3),
                 Ring(S, es, nc, "D_pm", [128, 512], F32, 1, psum=True), Ring(S, es, nc, "D_pq", [128, 512], F32, 1, psum=True),
                 Ring(S, es, nc, "D_st", [128, 3, 512], F32, 1), Ring(S, es, nc, "D_tp", [128, 512], F32, 3))
        for tg in range(cfg.NOWN // 512):
            t0 = tg * 512
            me, rme = mer.next()
            h, rh = hr.next()
            xb, rxb = xbr.next()
            S.dma("sp", me[:], T["mergedS"].rearrange("(c p) t -> p c t", p=128)[:, :, t0:t0 + 512], w=[rme])
            S.dma("sp", h[:], T["xT"].rearrange("(c p) t -> p c t", p=128)[:, :, 2048 + t0:2048 + t0 + 512], w=[rh])
            for m in range(16):
                ps, rps = pp.next()
                for k in range(16):
                    S.op("pe", lambda e, ps=ps, k=k, m=m, me=me: e.matmul(
                        ps[:], wo[:, k, m * 128:(m + 1) * 128], me[:, k, :], start=(k == 0), stop=(k == 15)), r=[rwo, rme], w=[rps])
                S.op("dve", lambda e, ps=ps, m=m, h=h: e.scalar_tensor_tensor(
                    h[:, m, :], h[:, m, :], ALPHA, ps[:], ALU.mult, ALU.add), r=[rps, rh], w=[rh])

            def ow(m, h=h, xb=xb, rh=rh, rxb=rxb):
                S.op("act", lambda e: e.activation(xb[:, m, :], h[:, m, :], AF.Copy), r=[rh], w=[rxb], group=("xb", id(xb)))
            layer_norm_T(nc, S, C, h, rh, 512, gb[:, 0, :], gb[:, 1, :], rgb, rings, ow)
            S.dma("sp", T["x1S"].rearrange("(c p) t -> p c t", p=128)[:, :, t0:t0 + 512], h[:], r=[rh])
            S.dma("sp", T["x1bS"].rearrange("(c p) t -> p c t", p=128)[:, :, t0:t0 + 512], xb[:], r=[rxb])
        S.flush(final_wait=True)


def phase_D1(nc, S, C, T, W, cfg):
    with ExitStack() as es:
        wq, rwq = sbt(S, es, nc, "Q_wq", [128, 16, 2048], BF16)
        sk, rsk = sbt(S, es, nc, "Q_sk", [128, 16, 128], BF16)
        cast_load(S, wq[:], W["w_peer_q"].rearrange("(k p) n -> p k n", p=128), rwq)
        S.dma("pool", sk[:], W["skT"].rearrange("m d k -> d m k"), w=[rsk])
        xbr = Ring(S, es, nc, "Q_xb", [128, 16, 512], BF16, 2)
        qar = Ring(S, es, nc, "Q_qa", [128, 16, 512], BF16, 1)
        pp = Ring(S, es, nc, "Q_pp", [128, 512], F32, 3, psum=True)
        scp = Ring(S, es, nc, "Q_scp", [128, 2048], F32, 1, psum=True)
        scs = Ring(S, es, nc, "Q_scs", [128, 2048], F32, 2)
        fl = 0
        for tg in range(cfg.NOWN // 512):
            t0 = tg * 512
            xb, rxb = xbr.next()
            qa, rqa = qar.next()
            S.dma("sp", xb[:], T["x1bS"].rearrange("(c p) t -> p c t", p=128)[:, :, t0:t0 + 512], w=[rxb])
            for m in range(16):
                ps, rps = pp.next()
                for k in range(16):
                    S.op("pe", lambda e, ps=ps, k=k, m=m, xb=xb: e.matmul(
                        ps[:], wq[:, k, m * 128:(m + 1) * 128], xb[:, k, :], start=(k == 0), stop=(k == 15)), r=[rwq, rxb], w=[rps])
                fl ^= 1
                if fl:
                    S.op("act", lambda e, ps=ps, m=m, qa=qa: e.activation(qa[:, m, :], ps[:], AF.Copy), r=[rps], w=[rqa])
                else:
                    S.op("dve", lambda e, ps=ps, m=m, qa=qa: e.tensor_copy(qa[:, m, :], ps[:]), r=[rps], w=[rqa])
            for tt in range(4):
                sp_, rsp = scp.next()
                for m in range(16):
                    S.op("pe", lambda e, sp_=sp_, m=m, tt=tt, qa=qa: e.matmul(
                        sp_[:, m * 128:(m + 1) * 128], qa[:, m, tt * 128:(tt + 1) * 128], sk[:, m, :], start=True, stop=True),
                        r=[rqa, rsk], w=[rsp])
                ss, rss = scs.next()
                S.op("act", lambda e, ss=ss, sp_=sp_: e.activation(ss[:], sp_[:], AF.Copy), r=[rsp], w=[rss])
                S.dma("sp", T["scS"][t0 + tt * 128:t0 + tt * 128 + 128, :], ss[:], r=[rss])
        S.flush(final_wait=True)


def top16_multi(S, items, rsrc, rtmp, rout, rmid):
    gid = ("t16", id(items))
    for (src, tmp, vals, idx) in items:
        S.op("dve", lambda e, src=src, vals=vals: e.max(vals[:, 0:8], src), r=[rsrc], w=[rmid], group=gid + (0,))
    for (src, tmp, vals, idx) in items:
        S.op("dve", lambda e, src=src, vals=vals, idx=idx: e.max_index(idx[:, 0:8], vals[:, 0:8], src), r=[rsrc, rmid], w=[rout], group=gid + (1,))
    for (src, tmp, vals, idx) in items:
        S.op("dve", lambda e, src=src, tmp=tmp, vals=vals: e.match_replace(tmp, vals[:, 0:8], src, NEG), r=[rsrc, rmid], w=[rtmp], group=gid + (2,))
    for (src, tmp, vals, idx) in items:
        S.op("dve", lambda e, tmp=tmp, vals=vals: e.max(vals[:, 8:16], tmp), r=[rtmp], w=[rmid], group=gid + (3,))
    for (src, tmp, vals, idx) in items:
        S.op("dve", lambda e, tmp=tmp, vals=vals, idx=idx: e.max_index(idx[:, 8:16], vals[:, 8:16], tmp), r=[rtmp, rmid], w=[rout], group=gid + (4,))


def phase_D2(nc, S, C, T, cfg):
    with ExitStack() as es:
        scr = Ring(S, es, nc, "R_sc", [128, 16, 128], F32, 2)
        s2r = Ring(S, es, nc, "R_s2", [128, 16, 128], F32, 1)
        svr = Ring(S, es, nc, "R_sv", [128, 256], F32, 2)
        siur = Ring(S, es, nc, "R_siu", [128, 256], U32, 2)
        sifr = Ring(S, es, nc, "R_sif", [128, 256], F32, 2)
        cdr = Ring(S, es, nc, "R_cd", [128, 8, 256], F32, 2)
        cd2r = Ring(S, es, nc, "R_cd2", [128, 8, 256], F32, 1)
        cvr = Ring(S, es, nc, "R_cv", [128, 128], F32, 2)
        ciur = Ring(S, es, nc, "R_ciu", [128, 128], U32, 2)
        exr = Ring(S, es, nc, "R_ex", [128, 128], F32, 2)
        smr = Ring(S, es, nc, "R_sm", [128, 8], F32, 2)
        hlur = Ring(S, es, nc, "R_hlu", [128, 2, 128], U32, 2)
        hlfr = Ring(S, es, nc, "R_hlf", [128, 2, 128], F32, 2)
        eqr = Ring(S, es, nc, "R_eq", [128, 2048], F32, 2)
        prr = Ring(S, es, nc, "R_pr", [128, 2048], F32, 2)
        trr = Ring(S, es, nc, "R_tr", [128, 3, 128], F32, 2)
        t3r = Ring(S, es, nc, "R_t3", [128, 3, 128], F32, 2)
        ptr = Ring(S, es, nc, "R_pt", [128, 512], F32, 2, psum=True)
        a1r = Ring(S, es, nc, "R_a1", [128, 128], BF16, 8)
        a2r = Ring(S, es, nc, "R_a2", [128, 128], BF16, 8)
        gpr = Ring(S, es, nc, "R_gp", [128, 512], F32, 4, psum=True)
        abr = Ring(S, es, nc, "R_ab", [128, 128], F32, 4)
        t3nr = Ring(S, es, nc, "R_t3n", [128, 128], F32, 2)
        gtr = Ring(S, es, nc, "R_gt", [128, 128, 128], BF16, 2)
        iota = C["iota"]
        for tl in range(cfg.NOWN // 128):
            t0 = tl * 128
            sc, rsc = scr.next()
            s2, rs2 = s2r.next()
            sv, rsv = svr.next()
            siu, rsiu = siur.next()
            sif, rsif = sifr.next()
            S.dma("sp", sc[:], T["scS"][t0:t0 + 128, :].rearrange("t (m k) -> t m k", k=128), w=[rsc])
            rmid = S.res()
            top16_multi(S, [(sc[:, j, :], s2[:, j, :], sv[:, j * 16:(j + 1) * 16], siu[:, j * 16:(j + 1) * 16]) for j in range(16)],
                        rsc, rs2, rsv, rmid)
            S.op("dve", lambda e, sif=sif, siu=siu: e.tensor_copy(sif[:], siu[:]), r=[rsv, rmid], w=[rsif])
            cd, rcd = cdr.next()
            cd2, rcd2 = cd2r.next()
            S.op("dve", lambda e, cd=cd, sv=sv: e.tensor_tensor(
                bass.AP(cd, 0, [[2048, 128], [256, 8], [16, 16], [1, 16]]),
                bass.AP(sv, 0, [[256, 128], [32, 8], [1, 16], [0, 16]]),
                bass.AP(sv, 16, [[256, 128], [32, 8], [0, 16], [1, 16]]), ALU.add), r=[rsv, rmid], w=[rcd])
            cv, rcv = cvr.next()
            ciu, rciu = ciur.next()
            rmid2 = S.res()
            top16_multi(S, [(cd[:, h, :], cd2[:, h, :], cv[:, h * 16:(h + 1) * 16], ciu[:, h * 16:(h + 1) * 16]) for h in range(8)],
                        rcd, rcd2, rcv, rmid2)
            ex, rex = exr.next()
            sm, rsm = smr.next()
            tr, rtr = trr.next()
            S.op("dve", lambda e, ex=ex, cv=cv: e.tensor_tensor(
                bass.AP(ex, 0, [[128, 128], [16, 8], [1, 16]]), bass.AP(cv, 0, [[128, 128], [16, 8], [1, 16]]),
                bass.AP(cv, 0, [[128, 128], [16, 8], [0, 16]]), ALU.subtract), r=[rcv, rmid2], w=[rex])
            S.op("act", lambda e, ex=ex: e.activation(ex[:], ex[:], AF.Exp), r=[rex], w=[rex])
            S.op("dve", lambda e, ex=ex, sm=sm: e.tensor_reduce(sm[:], bass.AP(ex, 0, [[128, 128], [16, 8], [1, 16]]), AX.X, ALU.add),
                 r=[rex], w=[rsm])
            S.op("dve", lambda e, sm=sm: e.reciprocal(sm[:], sm[:]), r=[rsm], w=[rsm])
            S.op("dve", lambda e, tr=tr, ex=ex, sm=sm: e.tensor_tensor(
                bass.AP(tr, 256, [[384, 128], [16, 8], [1, 16]]), bass.AP(ex, 0, [[128, 128], [16, 8], [1, 16]]),
                bass.AP(sm, 0, [[8, 128], [1, 8], [0, 16]]), ALU.mult), r=[rex, rsm], w=[rtr])
            hlu, rhlu = hlur.next()
            hlf, rhlf = hlfr.next()
            S.op("dve", lambda e, hlu=hlu, ciu=ciu: e.tensor_single_scalar(hlu[:, 0, :], ciu[:], 4, ALU.logical_shift_right), r=[rcv], w=[rhlu])
            S.op("dve", lambda e, hlu=hlu, ciu=ciu: e.tensor_single_scalar(hlu[:, 1, :], ciu[:], 15, ALU.bitwise_and), r=[rcv], w=[rhlu])
            S.op("dve", lambda e, hlf=hlf, hlu=hlu: e.tensor_copy(hlf[:], hlu[:]), r=[rhlu], w=[rhlf])
            for half in range(2):
                eq, req = eqr.next()
                pr, rpr = prr.next()
                S.op("dve", lambda e, eq=eq, hlf=hlf, half=half: e.tensor_tensor(
                    bass.AP(eq, 0, [[2048, 128], [16, 128], [1, 16]]),
                    bass.AP(iota, 0, [[128, 128], [0, 128], [1, 16]]),
                    bass.AP(hlf, half * 128, [[256, 128], [1, 128], [0, 16]]), ALU.is_equal), r=[rhlf, C["r_iota"]], w=[req])
                S.op("pool", lambda e, pr=pr, eq=eq, sif=sif, half=half: e.tensor_tensor(
                    bass.AP(pr, 0, [[2048, 128], [256, 8], [16, 16], [1, 16]]),
                    bass.AP(eq, 0, [[2048, 128], [256, 8], [16, 16], [1, 16]]),
                    bass.AP(sif, half * 16, [[256, 128], [32, 8], [0, 16], [1, 16]]), ALU.mult), r=[req, rsif], w=[rpr])
                S.op("dve", lambda e, tr=tr, pr=pr, half=half: e.tensor_reduce(
                    tr[:, half, :], bass.AP(pr, 0, [[2048, 128], [16, 128], [1, 16]]), AX.X, ALU.add), r=[rpr], w=[rtr])
            pt, rpt = ptr.next()
            t3, rt3 = t3r.next()
            for i in range(3):
                S.op("pe", lambda e, pt=pt, tr=tr, i=i: e.transpose(pt[:, i * 128:(i + 1) * 128], tr[:, i, :], C["ident"][:]),
                     r=[rtr, C["r_ident"]], w=[rpt])
            S.op("act", lambda e, t3=t3, pt=pt: e.activation(t3[:].rearrange("p a b -> p (a b)"), pt[:, 0:384], AF.Copy), r=[rpt], w=[rt3])
            t3n, rt3n = t3nr.next()
            S.op("dve", lambda e, t3n=t3n, t3=t3: e.tensor_scalar(t3n[:], t3[:, 0, :], -1.0, None, ALU.mult), r=[rt3], w=[rt3n])
            gt, rgt = gtr.next()
            toks = {}
            gps = {}

            def stage1(ti):
                a1, ra1 = a1r.next()
                a2, ra2 = a2r.next()
                ab, rab = abr.next()
                toks[ti] = (a1, ra1, a2, ra2, ab, rab)
                S.op("dve", lambda e, a1=a1, t3=t3, ti=ti: e.tensor_scalar(
                    a1[:], iota[:], t3[:, 1, ti:ti + 1], t3[:, 2, ti:ti + 1], ALU.is_equal, ALU.mult), r=[rt3, C["r_iota"]], w=[ra1])
                S.op("act", lambda e, ab=ab, t3n=t3n, ti=ti: e.activation(
                    ab[:], iota[:], AF.Abs, bias=t3n[:, ti:ti + 1]), r=[rt3n, C["r_iota"]], w=[rab])

            def stage2(ti):
                a1, ra1, a2, ra2, ab, rab = toks.pop(ti)
                tq, u = ti // 4, ti % 4
                if u == 0:
                    gps[tq] = gpr.next()
                gp, rgp = gps[tq]
                S.op("act", lambda e, a2=a2, ab=ab: e.activation(a2[:], ab[:], AF.Relu, bias=C["one"][:, 0:1], scale=-1.0),
                     r=[rab, C["r_one"]], w=[ra2])
                S.op("pe", lambda e, gp=gp, a1=a1, a2=a2, u=u: e.matmul(gp[:, u * 128:(u + 1) * 128], a1[:], a2[:], start=True, stop=True),
                     r=[ra1, ra2], w=[rgp])

            def evac(tq):
                gp, rgp = gps.pop(tq)
                S.op("act", lambda e, gt=gt, gp=gp, tq=tq: e.activation(
                    bass.AP(gt, tq * 4, [[16384, 128], [128, 128], [1, 4]]),
                    bass.AP(gp, 0, [[512, 128], [1, 128], [128, 4]]), AF.Copy), r=[rgp], w=[rgt])

            for ti in range(129):
                if ti < 128:
                    stage1(ti)
                if ti >= 1:
                    stage2(ti - 1)
                    if (ti - 1) % 4 == 3 and (ti - 1) // 4 >= 1:
                        evac((ti - 1) // 4 - 1)
            evac(31)
            S.dma("sp", T["GS"][:, :, t0:t0 + 128], gt[:], r=[rgt])
        S.flush(final_wait=True)


def phase_E(nc, S, C, T, W, cfg):
    for st in range(cfg.NOWN // 1024):
        T0 = st * 1024
        with ExitStack() as es:
            xb, rxb = sbt(S, es, nc, "E_xb", [128, 16, 1024], BF16)
            acc, racc = sbt(S, es, nc, "E_acc", [128, 16, 1024], F32)
            S.dma("sp", xb[:], T["x1bS"].rearrange("(c p) t -> p c t", p=128)[:, :, T0:T0 + 1024], w=[rxb])
            with ExitStack() as es2:
                ur = Ring(S, es2, nc, "E_u", [128, 16, 256], BF16, 3)
                vr = Ring(S, es2, nc, "E_v", [128, 2, 2048], BF16, 3)
                gr = Ring(S, es2, nc, "E_g", [128, 2, 1024], BF16, 3)
                atr = Ring(S, es2, nc, "E_at", [128, 2, 1024], BF16, 2)
                glr = Ring(S, es2, nc, "E_gl", [128, 512], BF16, 3)
                hp = Ring(S, es2, nc, "E_hp", [128, 512], F32, 4, psum=True)
                vp = Ring(S, es2, nc, "E_vp", [128, 512], F32, 4, psum=True)
                tiles = {}

                def load(s):
                    e0 = s * 256
                    u, ru = ur.next()
                    v, rv = vr.next()
                    g, rg = gr.next()
                    at, rat = atr.next()
                    S.dma("pool", u[:], W["peer_uT"].rearrange("(k p) e -> p k e", p=128)[:, :, e0:e0 + 256], w=[ru])
                    cast_load(S, v[:], W["peer_v"][e0:e0 + 256, :].rearrange("(c p) f -> p c f", p=128), rv)
                    S.dma("sp", g[:], T["GS"][:, 2 * s:2 * s + 2, T0:T0 + 1024], w=[rg])
                    tiles[s] = (u, ru, v, rv, g, rg, at, rat)

                def u_steps(s):
                    u, ru, v, rv, g, rg, at, rat = tiles[s]
                    for c in range(2):
                        pss = [hp.next(), hp.next()]
                        for k in range(16):
                            for half in range(2):
                                ps, rps = pss[half]
                                S.op("pe", lambda e, ps=ps, k=k, c=c, half=half, u=u: e.matmul(
                                    ps[:], u[:, k, c * 128:(c + 1) * 128], xb[:, k, half * 512:(half + 1) * 512],
                                    start=(k == 0), stop=(k == 15)), r=[ru, rxb], w=[rps])
                            if k % 2 == 1:
                                yield
                        for half in range(2):
                            ps, rps = pss[half]
                            gl, rgl = glr.next()
                            S.op("act", lambda e, gl=gl, ps=ps: e.activation(gl[:], ps[:], AF.Gelu), r=[rps], w=[rgl])
                            S.op("pool", lambda e, at=at, gl=gl, g=g, c=c, half=half: e.tensor_tensor(
                                at[:, c, half * 512:(half + 1) * 512], gl[:], g[:, c, half * 512:(half + 1) * 512], ALU.mult),
                                r=[rgl, rg], w=[rat])

                def v_steps(s):
                    u, ru, v, rv, g, rg, at, rat = tiles[s]
                    for m in range(16):
                        pss = [vp.next(), vp.next()]
                        for c in range(2):
                            for half in range(2):
                                ps, rps = pss[half]
                                S.op("pe", lambda e, ps=ps, c=c, m=m, half=half, v=v, at=at: e.matmul(
                                    ps[:], v[:, c, m * 128:(m + 1) * 128], at[:, c, half * 512:(half + 1) * 512],
                                    start=(c == 0), stop=(c == 1)), r=[rv, rat], w=[rps])
                        for half in range(2):
                            ps, rps = pss[half]
                            if s == 0:
                                S.op("dve", lambda e, ps=ps, m=m, half=half: e.tensor_copy(acc[:, m, half * 512:(half + 1) * 512], ps[:]),
                                     r=[rps], w=[racc], group="accw")
                            else:
                                S.op("dve", lambda e, ps=ps, m=m, half=half: e.tensor_tensor(
                                    acc[:, m, half * 512:(half + 1) * 512], ps[:], acc[:, m, half * 512:(half + 1) * 512], ALU.add),
                                    r=[rps], w=[racc], group="accw")
                        yield

                NS = 64
                load(0)
                load(1)
                for _ in u_steps(0):
                    pass
                for s in range(1, NS + 1):
                    if s + 1 < NS:
                        load(s + 1)
                    if s < NS:
                        ug = u_steps(s)
                    else:
                        ug = iter(())
                    vg = v_steps(s - 1)
                    for _ in vg:
                        next(ug, None)
                    for _ in ug:
                        pass
                    del tiles[s - 1]
                S.flush(final_wait=True)
            with ExitStack() as es3:
                gb, rgb = sbt(S, es3, nc, "E_gb", [128, 2, 16], F32)
                S.dma("sp", gb[:, 0, :], W["ln2_g"], w=[rgb])
                S.dma("sp", gb[:, 1, :], W["ln2_b"], w=[rgb])
                x1r = Ring(S, es3, nc, "E_x1", [128, 1024], F32, 2)
                rings = (Ring(S, es3, nc, "E_hb", [128, 512], BF16, 3), Ring(S, es3, nc, "E_hs", [128, 512], BF16, 3),
                         Ring(S, es3, nc, "E_pm", [128, 512], F32, 1, psum=True), Ring(S, es3, nc, "E_pq", [128, 512], F32, 1, psum=True),
                         Ring(S, es3, nc, "E_st", [128, 3, 512], F32, 1), Ring(S, es3, nc, "E_tp", [128, 512], F32, 3))
                for m in range(16):
                    x1, rx1 = x1r.next()
                    S.dma("sp", x1[:], T["x1S"][m * 128:(m + 1) * 128, T0:T0 + 1024], w=[rx1])
                    S.op("dve", lambda e, x1=x1, m=m: e.scalar_tensor_tensor(
                        acc[:, m, :], x1[:], ALPHA, acc[:, m, :], ALU.mult, ALU.add), r=[rx1, racc], w=[racc])
                for half in range(2):
                    hv = acc[:, :, half * 512:(half + 1) * 512]
                    layer_norm_T(nc, S, C, hv, racc, 512, gb[:, 0, :], gb[:, 1, :], rgb, rings, lambda m: None)
                S.dma("sp", T["outT"].rearrange("(c p) t -> p c t", p=128)[:, :, T0:T0 + 1024], acc[:], r=[racc])
                S.flush(final_wait=True)
PHASES = ("A", "B", "C1", "C2", "D1", "D2", "E")

class Cfg:
    def __init__(self, layer, nown, coff, mask_blk_col, corr_tg):
        self.layer = layer
        self.NOWN = nown
        self.NALL = nown + 2048
        self.coff = coff
        self.mask_blk_col = mask_blk_col
        self.corr_tg = corr_tg
        self.suffix = f"_L{layer}"

    def mask_qb(self, r):
        return (self.mask_blk_col // r) // 128


def scratch_shapes(cfg):
    n, a = cfg.NOWN, cfg.NALL
    return {
        "qS": ([1536, n], BF16), "kS": ([1536, a], BF16), "vS": ([a, 1536], BF16),
        "xpS": ([1024, 16 + n], F32), "gS": ([4096, n], BF16), "attnS": ([512, n], BF16),
        "mergedS": ([2048, n], BF16), "x1S": ([2048, n], F32), "x1bS": ([2048, n], BF16),
        "scS": ([n, 2048], F32), "GS": ([128, 128, n], BF16),
    }


INPUTS_T = {
    "cosT": [128, 8192], "sinT": [128, 8192], "ident": [128, 128], "iota": [128, 128],
    "pswap": [128, 128], "masks": [128, 2, 256], "flag": [128, 1], "corr": [128, 4, 16],
}
INPUTS_W = {
    "w_in": [2048, 9728], "b_gate": [128, 32], "w_branch_attn": [512, 2048], "w_branch_pool": [1024, 2048],
    "w_pool_group": [4, 256, 256], "pool_scale": [128, 8], "w_out": [2048, 2048], "ln1_g": [128, 16], "ln1_b": [128, 16],
    "w_peer_q": [2048, 2048], "skT": [16, 128, 128], "peer_uT": [2048, 16384], "peer_v": [16384, 2048],
    "ln2_g": [128, 16], "ln2_b": [128, 16],
}


class LazyT(dict):
    def __init__(self, nc, cfg, shared, used_inputs, x_in=None, x_out=None):
        super().__init__()
        self.nc = nc
        self.cfg = cfg
        self.shared = shared
        self.used_inputs = used_inputs
        self.sc = scratch_shapes(cfg)
        if x_in is not None:
            self["xT"] = x_in
        if x_out is not None:
            self["outT"] = x_out

    def __missing__(self, k):
        nc = self.nc
        sfx = self.cfg.suffix
        if k in self.sc:
            shape, dt = self.sc[k]
            v = nc.dram_tensor(k + sfx, shape, dt, kind="Internal").ap()
        elif k in INPUTS_T:
            if k not in self.shared:
                self.shared[k] = nc.dram_tensor(k, INPUTS_T[k], F32, kind="ExternalInput").ap()
                self.used_inputs.append(k)
            v = self.shared[k]
        elif k in INPUTS_W:
            v = nc.dram_tensor(k + sfx, INPUTS_W[k], F32, kind="ExternalInput").ap()
            self.used_inputs.append(k + sfx)
        else:
            raise KeyError(k)
        self[k] = v
        return v


def emit_layer(nc, S, C, T, cfg):
    phase_A(nc, S, C, T, T, cfg)
    phase_B(nc, S, C, T, cfg)
    phase_C1(nc, S, C, T, T, cfg)
    phase_C2(nc, S, C, T, T, cfg)
    phase_D1(nc, S, C, T, T, cfg)
    phase_D2(nc, S, C, T, cfg)
    phase_E(nc, S, C, T, T, cfg)


def build_program():
    nc = bass.Bass("TRN2", target_bir_lowering=False)
    used = ["xT"]
    shared = {}
    x0 = nc.dram_tensor("xT", [2048, 8192], F32, kind="ExternalInput").ap()
    xmid = nc.dram_tensor("xmid", [2048, 6144], F32, kind="Internal").ap()
    out = nc.dram_tensor("outT", [2048, 4096], F32, kind="ExternalOutput").ap()
    cfg0 = Cfg(0, 6144, 0, 2048, 4)
    cfg1 = Cfg(1, 4096, 2048, 0, 0)
    with ExitStack() as es:
        S = Sched(nc, es)
        T0 = LazyT(nc, cfg0, shared, used, x_in=x0, x_out=xmid)
        T1 = LazyT(nc, cfg1, shared, used, x_in=xmid, x_out=out)
        C = load_consts(nc, S, es, T0)
        S.flush(final_wait=True)
        emit_layer(nc, S, C, T0, cfg0)
        emit_layer(nc, S, C, T1, cfg1)
    return nc, used


def rope_tables_np(pos):
    inv = (np.float32(10000.0) ** (-np.arange(0, 128, 2, dtype=np.float32) / np.float32(128))).astype(np.float32)
    ang = pos.astype(np.float32)[:, None] * inv[None, :]
    ang = np.concatenate([ang, ang], axis=-1)
    cos = np.cos(ang).astype(np.float32).T
    sin = np.sin(ang).astype(np.float32).T
    sgn = np.where(np.arange(128) < 64, -1.0, 1.0).astype(np.float32)[:, None]
    return np.ascontiguousarray(cos), np.ascontiguousarray(sin * sgn)


def const_inputs(core):
    ci = core % 4
    flag = 1.0 if ci > 0 else 0.0
    pos = np.maximum(ci * 4096 - 4096 + np.arange(8192), 0)
    cosT, sinT = rope_tables_np(pos)
    ident = np.eye(128, dtype=np.float32)
    iota = np.tile(np.arange(128, dtype=np.float32)[None, :], (128, 1))
    pswap = np.zeros((128, 128), np.float32)
    pswap[np.arange(128), (np.arange(128) + 64) % 128] = 1.0
    kk = np.arange(128)[:, None]
    qq = np.arange(128)[None, :]
    prev = (kk >= qq).astype(np.float32)
    cur = (kk <= qq).astype(np.float32)
    masks = np.zeros((128, 2, 256), np.float32)
    masks[:, 0, :128] = prev
    masks[:, 0, 128:] = cur
    masks[:, 1, :128] = prev * flag
    masks[:, 1, 128:] = cur
    corr = np.ones((128, 4, 16), np.float32)
    if ci == 0:
        t = np.arange(16)
        for gi, w in enumerate((2, 4, 8, 16)):
            corr[:, gi, :] = (w / np.minimum(t + 1, w)).astype(np.float32)[None, :]
    return {"cosT": cosT, "sinT": sinT, "ident": ident, "iota": iota, "pswap": pswap, "masks": masks,
            "flag": np.full((128, 1), flag, np.float32), "corr": corr}


def colvec(v, n):
    return np.ascontiguousarray(np.asarray(v, np.float32).reshape(n, 128).T)


def layer_weights(inp, l):
    return {
        "w_in": np.ascontiguousarray(inp["w_in"][l]),
        "b_gate": colvec(inp["b_gate"][l].reshape(-1), 32),
        "w_branch_attn": np.ascontiguousarray(inp["w_branch_attn"][l]),
        "w_branch_pool": np.ascontiguousarray(inp["w_branch_pool"][l]),
        "w_pool_group": np.ascontiguousarray(inp["w_pool_group"][l]),
        "pool_scale": colvec(inp["pool_scale"][l], 8),
        "w_out": np.ascontiguousarray(inp["w_out"][l]),
        "ln1_g": colvec(inp["ln1_g"][l], 16), "ln1_b": colvec(inp["ln1_b"][l], 16),
        "w_peer_q": np.ascontiguousarray(inp["w_peer_q"][l]),
        "skT": np.ascontiguousarray(inp["peer_subkeys"][l].reshape(16, 128, 128).transpose(0, 2, 1)),
        "peer_uT": np.ascontiguousarray(inp["peer_u"][l].T),
        "peer_v": np.ascontiguousarray(inp["peer_v"][l]),
        "ln2_g": colvec(inp["ln2_g"][l], 16), "ln2_b": colvec(inp["ln2_b"][l], 16),
    }


def make_xT(xfull, core):
    b, ci = core // 4, core % 4
    s0 = ci * 4096
    xT = np.zeros((2048, 8192), np.float32)
    xT[:, 4096:] = xfull[b, s0:s0 + 4096].T
    if ci > 0:
        xT[:, :4096] = xfull[b, s0 - 4096:s0].T
    return xT


_PROG = []


def kernel(**inp):
    x = np.asarray(inp["x"], np.float32)
    if not _PROG:
        _PROG.append(build_program())
    nc, used = _PROG[0]
    wl = [layer_weights(inp, l) for l in range(2)]
    in_maps = []
    for c in range(8):
        m = dict(const_inputs(c))
        m["xT"] = make_xT(x, c)
        for l in range(2):
            for k, v in wl[l].items():
                m[f"{k}_L{l}"] = v
        in_maps.append({k: m[k] for k in used})
    res = run_bass_kernel_spmd(nc, in_maps, core_ids=list(range(8))).results
    out = np.empty_like(x)
    for c in range(8):
        b, ci = c // 4, c % 4
        out[b, ci * 4096:(ci + 1) * 4096] = res[c]["outT"].T
    return out
```

```python
from contextlib import ExitStack
from concourse.bass_utils import run_bass_kernel_spmd
import numpy as np
import concourse.bass as bass
import concourse.mybir as mybir

F32 = mybir.dt.float32
BF16 = mybir.dt.bfloat16
I32 = mybir.dt.int32
U32 = mybir.dt.uint32
AF = mybir.ActivationFunctionType
ALU = mybir.AluOpType

COMPUTE = ("pe", "act", "dve", "pool")
ENGINES = ("pe", "act", "dve", "pool", "sp")


class Res:
    __slots__ = ("name", "w", "r", "excl")

    def __init__(self, name, excl=False):
        self.name = name
        self.excl = excl
        self.w = None
        self.r = []


class Op:
    __slots__ = ("eng", "fn", "deps", "is_dma", "seq", "sem", "has_dep", "idx", "group")

    def __init__(self, eng, fn, is_dma):
        self.eng = eng
        self.fn = fn
        self.deps = []
        self.is_dma = is_dma
        self.seq = None
        self.sem = None
        self.has_dep = False
        self.group = None


class Sched:
    NDMA_SEM = 6

    def __init__(self, nc, es, same_engine_sync=True):
        self.nc = nc
        self.ops = []
        self.same_engine_sync = same_engine_sync
        self.sem = {e: es.enter_context(nc.semaphore("c_" + e)) for e in ENGINES}
        self.cnt = {e: 0 for e in ENGINES}
        self.dsem = {e: [es.enter_context(nc.semaphore(f"d_{e}{i}")) for i in range(self.NDMA_SEM)]
                     for e in ("sp", "act", "pool", "pe", "dve")}
        self.dcnt = {e: [0] * self.NDMA_SEM for e in self.dsem}
        self.drr = {e: 0 for e in self.dsem}
        self.dlast = {e: [None] * self.NDMA_SEM for e in self.dsem}
        self.waited = {}
        self.pending_dma = []
        self.nres = 0

    def res(self, name=None, excl=False):
        self.nres += 1
        return Res(name or f"r{self.nres}", excl)

    def _track(self, op, r, w):
        deps = set()
        g = op.group
        xr = [x for x in r if x.excl and x not in w]
        r = [x for x in r if not (x.excl and x not in w)]
        w = list(w) + xr
        for x in r:
            for o in (x.w or ()):
                deps.add(o)
        for x in w:
            for o in (x.w or ()):
                if g is None or o.group != g:
                    deps.add(o)
            for o in x.r:
                deps.add(o)
        deps.discard(op)
        op.deps = list(deps)
        for x in r:
            x.r.append(op)
        for x in w:
            if g is not None and x.w and x.w[-1].group == g and not x.r:
                x.w.append(op)
            else:
                x.w = [op]
            x.r = []

    def op(self, eng, fn, r=(), w=(), group=None):
        o = Op(eng, fn, False)
        o.group = group
        self._track(o, r, w)
        self.ops.append(o)
        return o

    def dma(self, eng, out, in_, r=(), w=(), group=None, **kw):
        if eng == "pool":
            kw = dict(kw)
            kw.setdefault("max_dma_last_dim", 4096)

        def fn(e, out=out, in_=in_, kw=kw):
            return e.dma_start(out=out, in_=in_, **kw)
        o = Op(eng, fn, True)
        o.group = group
        self._track(o, r, w)
        self.ops.append(o)
        return o

    def _eng(self, name):
        nc = self.nc
        return {"pe": nc.tensor, "act": nc.scalar, "dve": nc.vector, "pool": nc.gpsimd, "sp": nc.sync}[name]

    def flush(self, final_wait=False):
        nc = self.nc
        ops = self.ops
        self.ops = []
        if not ops:
            return
        for o in ops:
            for d in o.deps:
                d.has_dep = True
        per = {e: [] for e in ENGINES}
        for o in ops:
            waits = []
            e = o.eng
            if o.is_dma:
                k = self.drr[e]
                self.drr[e] = (k + 1) % (3 if e == "pool" else self.NDMA_SEM)
                prev = self.dlast[e][k]
                if prev is not None:
                    waits.append((prev.sem, prev.seq))
                self.dcnt[e][k] += 16
                o.sem = self.dsem[e][k]
                o.seq = self.dcnt[e][k]
                self.dlast[e][k] = o
            else:
                if o.has_dep or True:
                    self.cnt[e] += 1
                    o.sem = self.sem[e]
                    o.seq = self.cnt[e]
            for d in o.deps:
                if d.seq is None:
                    raise RuntimeError("dep not scheduled")
                if (not d.is_dma) and d.eng == e:
                    if e == "pe" or not self.same_engine_sync:
                        continue
                waits.append((d.sem, d.seq))
            mx = {}
            for s, v in waits:
                key = id(s)
                if key not in mx or mx[key][1] < v:
                    mx[key] = (s, v)
            fin = []
            for key, (s, v) in mx.items():
                wk = (e, key)
                if self.waited.get(wk, 0) >= v:
                    continue
                self.waited[wk] = v
                fin.append((s, v))
            per[e].append((o, fin))

        if final_wait:
            self.final_waits = [(self.dsem[e][k], self.dcnt[e][k]) for e in self.dsem
                                for k in range(self.NDMA_SEM) if self.dcnt[e][k] > 0]
        else:
            self.final_waits = []

        with nc.Block() as block:
            def mk(ename):
                lst = per[ename]
                fw = self.final_waits if ename == "sp" else []

                def body(eng):
                    for o, waits in lst:
                        for s, v in waits:
                            eng.wait_ge(s, v)
                        ins = o.fn(eng)
                        if o.is_dma:
                            ins.then_inc(o.sem, 16)
                        else:
                            ins.then_inc(o.sem, 1)
                    for s, v in fw:
                        eng.wait_ge(s, v)
                return body
            for ename, reg in (("pe", block.tensor), ("act", block.scalar), ("dve", block.vector),
                               ("pool", block.gpsimd), ("sp", block.sync)):
                if per[ename] or (ename == "sp" and self.final_waits):
                    reg(mk(ename))
        if final_wait:
            self.pending_dma = []
AX = mybir.AxisListType
ALPHA = float(4 ** 0.25)
NEG = -1.0e30


_UID = [0]


class Ring:
    def __init__(self, S, es, nc, name, shape, dtype, n, psum=False):
        self.t = []
        _UID[0] += 1
        name = f"{name}u{_UID[0]}"
        for i in range(n):
            mk = nc.psum_tensor if psum else nc.sbuf_tensor
            self.t.append((es.enter_context(mk(f"{name}_{i}", shape, dtype)), S.res(f"{name}{i}", excl=psum)))
        self.i = 0

    def next(self):
        x = self.t[self.i % len(self.t)]
        self.i += 1
        return x


def sbt(S, es, nc, name, shape, dtype):
    _UID[0] += 1
    name = f"{name}u{_UID[0]}"
    return es.enter_context(nc.sbuf_tensor(name, shape, dtype)), S.res(name)


_GRP = [0]


def cast_load(S, dst, src, rdst):
    K_, N_ = dst.shape[1], dst.shape[2]
    _GRP[0] += 1
    if N_ <= 1024:
        S.dma("pool", dst, src, w=[rdst])
    else:
        for k in range(K_):
            S.dma("pool", dst[:, k, :], src[:, k, :], w=[rdst], group=("cl", _GRP[0]))


def load_consts(nc, S, es, T):
    C = {}
    C["ident"], C["r_ident"] = sbt(S, es, nc, "c_ident", [128, 128], F32)
    C["iota"], C["r_iota"] = sbt(S, es, nc, "c_iota", [128, 128], F32)
    C["pswap"], C["r_pswap"] = sbt(S, es, nc, "c_pswap", [128, 128], BF16)
    C["onesb"], C["r_onesb"] = sbt(S, es, nc, "c_onesb", [128, 128], BF16)
    C["onesd"], C["r_onesd"] = sbt(S, es, nc, "c_onesd", [128, 128], BF16)
    C["masks"], C["r_masks"] = sbt(S, es, nc, "c_masks", [128, 2, 256], F32)
    C["flag"], C["r_flag"] = sbt(S, es, nc, "c_flag", [128, 1], F32)
    C["corr"], C["r_corr"] = sbt(S, es, nc, "c_corr", [128, 4, 16], F32)
    S.dma("sp", C["ident"][:], T["ident"], w=[C["r_ident"]])
    S.dma("sp", C["iota"][:], T["iota"], w=[C["r_iota"]])
    S.dma("pool", C["pswap"][:], T["pswap"], w=[C["r_pswap"]])
    S.dma("sp", C["masks"][:], T["masks"], w=[C["r_masks"]])
    S.dma("sp", C["flag"][:], T["flag"], w=[C["r_flag"]])
    S.dma("sp", C["corr"][:], T["corr"], w=[C["r_corr"]])
    C["masksb"], C["r_masksb"] = sbt(S, es, nc, "c_masksb", [128, 2, 256], BF16)
    S.op("dve", lambda e: e.tensor_copy(C["masksb"][:], C["masks"][:]), r=[C["r_masks"]], w=[C["r_masksb"]])
    S.op("dve", lambda e: e.memset(C["onesb"][:], 1.0), w=[C["r_onesb"]])
    S.op("dve", lambda e: e.memset(C["onesd"][:], 1.0 / 2048.0), w=[C["r_onesd"]])
    return C


def phase_A(nc, S, C, T, W, cfg):
    with ExitStack() as es:
        xsb, rx = sbt(S, es, nc, "A_x", [128, 16, 2048], BF16)
        cs, rcs = sbt(S, es, nc, "A_cs", [128, 2, 2048], F32)
        bg, rbg = sbt(S, es, nc, "A_bg", [128, 32], F32)
        wring = Ring(S, es, nc, "A_w", [128, 16, 512], BF16, 2)
        pring = Ring(S, es, nc, "A_ps", [128, 512], F32, 4, psum=True)
        p2ring = Ring(S, es, nc, "A_ps2", [128, 512], F32, 2, psum=True)
        zring = Ring(S, es, nc, "A_zb", [128, 512], BF16, 2)
        t1ring = Ring(S, es, nc, "A_t1", [128, 512], F32, 2)
        t2ring = Ring(S, es, nc, "A_t2", [128, 512], F32, 2)
        string = Ring(S, es, nc, "A_st", [128, 512], BF16, 4)
        sfring = Ring(S, es, nc, "A_sf", [128, 512], F32, 2)
        S.dma("sp", bg[:], W["b_gate"], w=[rbg])
        flip = [0]
        for sg in range(cfg.NALL // 2048):
            cast_load(S, xsb[:], T["xT"].rearrange("(k p) t -> p k t", p=128)[:, :, sg * 2048:(sg + 1) * 2048], rx)
            S.dma("sp", cs[:, 0, :], T["cosT"][:, cfg.coff + sg * 2048:cfg.coff + (sg + 1) * 2048], w=[rcs])
            S.dma("sp", cs[:, 1, :], T["sinT"][:, cfg.coff + sg * 2048:cfg.coff + (sg + 1) * 2048], w=[rcs])
            blocks = [3, 4, 5, 6, 7, 8, 9, 10] if sg == 0 else list(range(19))
            for blk in blocks:
                wb, rw = wring.next()
                S.dma("pool", wb[:], W["w_in"].rearrange("(k p) c -> p k c", p=128)[:, :, blk * 512:(blk + 1) * 512], w=[rw])
                subs = [0, 1, 2, 3] if (sg > 0 or blk in (5, 8)) else [3]
                for sub in subs:
                    j0 = sg * 2048 + sub * 512
                    t0 = j0 - 2048
                    if 6 <= blk <= 8:
                        for tt in range(4):
                            ps, rps = pring.next()
                            for k in range(16):
                                S.op("pe", lambda e, ps=ps, k=k, wb=wb, c0=sub * 512 + tt * 128: e.matmul(
                                    ps[:], xsb[:, k, c0:c0 + 128], wb[:, k, :], start=(k == 0), stop=(k == 15)),
                                    r=[rx, rw], w=[rps])
                            st, rst = string.next()
                            flip[0] ^= 1
                            if flip[0]:
                                S.op("act", lambda e, st=st, ps=ps: e.activation(st[:], ps[:], AF.Copy), r=[rps], w=[rst])
                            else:
                                S.op("dve", lambda e, st=st, ps=ps: e.tensor_copy(st[:], ps[:]), r=[rps], w=[rst])
                            S.dma("sp", T["vS"][j0 + tt * 128:j0 + tt * 128 + 128, (blk - 6) * 512:(blk - 5) * 512], st[:], r=[rst])
                        continue
                    for mc in range(4):
                        ps, rps = pring.next()
                        for k in range(16):
                            S.op("pe", lambda e, ps=ps, k=k, wb=wb, mc=mc, sub=sub: e.matmul(
                                ps[:], wb[:, k, mc * 128:(mc + 1) * 128], xsb[:, k, sub * 512:(sub + 1) * 512],
                                start=(k == 0), stop=(k == 15)), r=[rx, rw], w=[rps])
                        if blk < 6:
                            zb, rzb = zring.next()
                            ps2, rps2 = p2ring.next()
                            t1, rt1 = t1ring.next()
                            t2, rt2 = t2ring.next()
                            st, rst = string.next()
                            S.op("act", lambda e, zb=zb, ps=ps: e.activation(zb[:], ps[:], AF.Copy), r=[rps], w=[rzb])
                            S.op("pe", lambda e, ps2=ps2, zb=zb: e.matmul(ps2[:], C["pswap"][:], zb[:], start=True, stop=True),
                                 r=[rzb, C["r_pswap"]], w=[rps2])
                            S.op("dve", lambda e, t1=t1, ps=ps, sub=sub: e.tensor_tensor(
                                t1[:], ps[:], cs[:, 0, sub * 512:(sub + 1) * 512], ALU.mult), r=[rps, rcs], w=[rt1])
                            S.op("dve", lambda e, t2=t2, ps2=ps2, sub=sub: e.tensor_tensor(
                                t2[:], ps2[:], cs[:, 1, sub * 512:(sub + 1) * 512], ALU.mult), r=[rps2, rcs], w=[rt2])
                            S.op("pool", lambda e, st=st, t1=t1, t2=t2: e.tensor_tensor(st[:], t1[:], t2[:], ALU.add),
                                 r=[rt1, rt2], w=[rst])
                            if blk < 3:
                                row = (blk * 4 + mc) * 128
                                S.dma("sp", T["qS"][row:row + 128, t0:t0 + 512], st[:], r=[rst])
                            else:
                                row = ((blk - 3) * 4 + mc) * 128
                                S.dma("sp", T["kS"][row:row + 128, j0:j0 + 512], st[:], r=[rst])
                        elif blk < 11:
                            sf, rsf = sfring.next()
                            row = ((blk - 9) * 4 + mc) * 128
                            if sg == 0:
                                S.op("act", lambda e, sf=sf, ps=ps: e.activation(sf[:, 0:16], ps[:, 496:512], AF.Copy, scale=C["flag"][:, 0:1]),
                                     r=[rps, C["r_flag"]], w=[rsf])
                                S.dma("sp", T["xpS"][row:row + 128, 0:16], sf[:, 0:16], r=[rsf])
                            else:
                                S.op("act", lambda e, sf=sf, ps=ps: e.activation(sf[:], ps[:], AF.Copy), r=[rps], w=[rsf])
                                S.dma("sp", T["xpS"][row:row + 128, 16 + t0:16 + t0 + 512], sf[:], r=[rsf])
                        else:
                            c = (blk - 11) * 4 + mc
                            st, rst = string.next()
                            S.op("act", lambda e, st=st, ps=ps, c=c: e.activation(st[:], ps[:], AF.Sigmoid, bias=bg[:, c:c + 1]),
                                 r=[rps, rbg], w=[rst])
                            S.dma("sp", T["gS"][c * 128:(c + 1) * 128, t0:t0 + 512], st[:], r=[rst])
        S.flush(final_wait=True)


def phase_B(nc, S, C, T, cfg):
    NOWN, NALL = cfg.NOWN, cfg.NALL
    scale = float(128 ** -0.5)
    with ExitStack() as es:
        kring = Ring(S, es, nc, "B_k", [128, NALL], BF16, 2)
        qring = Ring(S, es, nc, "B_q", [128, NOWN], BF16, 2)
        vring = Ring(S, es, nc, "B_v", [128, NALL // 128, 128], BF16, 2)
        ndring = Ring(S, es, nc, "B_nd", [128, 2, NOWN], F32, 1)
        aring = Ring(S, es, nc, "B_at", [128, NOWN], BF16, 1)
        psS = Ring(S, es, nc, "B_pss", [128, 512], F32, 3, psum=True)
        psO = Ring(S, es, nc, "B_pso", [128, 512], F32, 3, psum=True)
        pT1 = Ring(S, es, nc, "B_p1", [128, 256], BF16, 3)
        pT2 = Ring(S, es, nc, "B_p2", [128, 256], BF16, 3)
        vrows = T["vS"]
        for h in range(4):
            nd, rnd = ndring.next()
            for g, r in enumerate((1, 4, 16)):
                hh = g * 4 + h
                kt, rk = kring.next()
                qt, rq = qring.next()
                vt, rv = vring.next()
                S.dma("sp", kt[:], T["kS"][hh * 128:(hh + 1) * 128, :], w=[rk])
                S.dma("sp", qt[:], T["qS"][hh * 128:(hh + 1) * 128, :], w=[rq])
                nkb = (NALL // 128) // r
                for p in range(r):
                    src = bass.AP(vrows.tensor, vrows.offset + p * 1536 + hh * 128,
                                  [[r * 1536, 128], [128 * r * 1536, nkb], [1, 128]])
                    S.dma("sp", vt[:, p * nkb:(p + 1) * nkb, :], src, w=[rv])
                nqb = (NOWN // 128) // r
                off = 16 // r
                for p in range(r):
                    for qb in range(nqb):
                        kb = qb + off
                        ps, rps = psS.next()
                        po, rpo = psO.next()
                        p1, rp1 = pT1.next()
                        p2, rp2 = pT2.next()
                        qs = (qb * 128) * r + p
                        qap = qt[:, qs:qs + 127 * r + 1:r]
                        for i, kk in enumerate((kb - 1, kb)):
                            ks = (kk * 128) * r + p
                            S.op("pe", lambda e, ps=ps, i=i, ks=ks, kt=kt, qap=qap, r=r: e.matmul(
                                ps[:, i * 128:(i + 1) * 128], kt[:, ks:ks + 127 * r + 1:r], qap, start=True, stop=True),
                                r=[rk, rq], w=[rps])
                        S.op("act", lambda e, p1=p1, ps=ps: e.activation(p1[:], ps[:, 0:256], AF.Exp, scale=scale), r=[rps], w=[rp1])
                        mi = 1 if qb == cfg.mask_qb(r) else 0
                        S.op("pool" if (qb + p) % 2 == 0 else "dve", lambda e, p2=p2, p1=p1, mi=mi: e.tensor_tensor(
                            p2[:], p1[:], C["masksb"][:, mi, :], ALU.mult), r=[rp1, C["r_masksb"]], w=[rp2])
                        for i, kk in enumerate((kb - 1, kb)):
                            S.op("pe", lambda e, po=po, i=i, kk=kk, vt=vt, p2=p2, p=p, nkb=nkb: e.matmul(
                                po[:, 0:128], vt[:, p * nkb + kk, :], p2[:, i * 128:(i + 1) * 128], start=(i == 0), stop=(i == 1)),
                                r=[rv, rp2], w=[rpo])
                        for i in range(2):
                            S.op("pe", lambda e, po=po, i=i, p2=p2: e.matmul(
                                po[:, 128:256], C["onesb"][:], p2[:, i * 128:(i + 1) * 128], start=(i == 0), stop=(i == 1)),
                                r=[C["r_onesb"], rp2], w=[rpo])
                        ndv = bass.AP(nd, qs, [[2 * NOWN, 128], [NOWN, 2], [r, 128]])
                        pov = bass.AP(po, 0, [[512, 128], [128, 2], [1, 128]])
                        if g == 0:
                            S.op("dve", lambda e, ndv=ndv, pov=pov: e.tensor_copy(ndv, pov), r=[rpo], w=[rnd], group=("nd", h))
                        else:
                            S.op("dve", lambda e, ndv=ndv, pov=pov: e.tensor_tensor(ndv, pov, ndv, ALU.add), r=[rpo], w=[rnd], group=("nd", h))
            at, rat = aring.next()
            S.op("dve", lambda e, nd=nd: e.reciprocal(nd[:, 1, :], nd[:, 1, :]), r=[rnd], w=[rnd])
            S.op("dve", lambda e, nd=nd, at=at: e.tensor_tensor(at[:], nd[:, 0, :], nd[:, 1, :], ALU.mult), r=[rnd], w=[rat])
            S.dma("sp", T["attnS"][h * 128:(h + 1) * 128, :], at[:], r=[rat])
        S.flush(final_wait=True)


def phase_C1(nc, S, C, T, W, cfg):
    NOWN = cfg.NOWN
    with ExitStack() as es:
        wbp, rwbp = sbt(S, es, nc, "C_wbp", [128, 8, 2048], BF16)
        wba, rwba = sbt(S, es, nc, "C_wba", [128, 4, 2048], BF16)
        wpg, rwpg = sbt(S, es, nc, "C_wpg", [128, 4, 2, 256], BF16)
        psc, rpsc = sbt(S, es, nc, "C_psc", [128, 8], F32)
        cast_load(S, wbp[:], W["w_branch_pool"].rearrange("(k p) n -> p k n", p=128), rwbp)
        cast_load(S, wba[:], W["w_branch_attn"].rearrange("(k p) n -> p k n", p=128), rwba)
        for g_ in range(4):
            S.dma("pool", wpg[:, g_, :, :], W["w_pool_group"][g_].rearrange("(k p) d -> p k d", p=128), w=[rwpg])
        S.dma("sp", psc[:], W["pool_scale"], w=[rpsc])
        xpr = Ring(S, es, nc, "C_xp", [128, 8, 528], F32, 1)
        tA = Ring(S, es, nc, "C_tA", [128, 2, 528], F32, 2)
        tB = Ring(S, es, nc, "C_tB", [128, 2, 528], F32, 2)
        poolr = Ring(S, es, nc, "C_pl", [128, 8, 512], BF16, 1)
        mixr = Ring(S, es, nc, "C_mx", [128, 8, 512], BF16, 1)
        atr = Ring(S, es, nc, "C_at", [128, 4, 512], BF16, 2)
        gr = Ring(S, es, nc, "C_g", [128, 2, 512], BF16, 4)
        mer = Ring(S, es, nc, "C_me", [128, 16, 512], BF16, 2)
        tu = Ring(S, es, nc, "C_tu", [128, 2, 512], F32, 3)
        pm = Ring(S, es, nc, "C_pm", [128, 512], F32, 2, psum=True)
        pa = Ring(S, es, nc, "C_pa", [128, 512], F32, 2, psum=True)
        pb = Ring(S, es, nc, "C_pb", [128, 512], F32, 2, psum=True)
        for tg in range(NOWN // 512):
            t0 = tg * 512
            xp, rxp = xpr.next()
            S.dma("sp", xp[:], T["xpS"].rearrange("(c p) t -> p c t", p=128)[:, :, t0:t0 + 528], w=[rxp])
            at, rat = atr.next()
            S.dma("sp", at[:], T["attnS"].rearrange("(c p) t -> p c t", p=128)[:, :, t0:t0 + 512], w=[rat])
            pl, rpl = poolr.next()
            mx, rmx = mixr.next()
            me, rme = mer.next()
            for gi in range(4):
                w = 2 << gi
                cur = xp[:, 2 * gi:2 * gi + 2, :]
                rcur = rxp
                lo = 0
                sh = 1
                eng = ("dve", "pool")
                for step in range(gi + 1):
                    dst, rdst = (tA if step % 2 == 0 else tB).next()
                    n = 528 - (lo + sh)
                    S.op(eng[(gi + step) % 2], lambda e, dst=dst, cur=cur, lo=lo, sh=sh, n=n: e.tensor_tensor(
                        dst[:, :, lo + sh:lo + sh + n], cur[:, :, lo + sh:lo + sh + n], cur[:, :, lo:lo + n], ALU.add),
                        r=[rcur], w=[rdst])
                    cur, rcur = dst, rdst
                    lo += sh
                    sh *= 2
                if tg == cfg.corr_tg:
                    S.op("dve", lambda e, cur=cur, gi=gi: e.tensor_tensor(
                        cur[:, :, 16:32], cur[:, :, 16:32], C["corr"][:, gi:gi + 1, :].to_broadcast([128, 2, 16]), ALU.mult),
                        r=[rcur, C["r_corr"]], w=[rcur])
                S.op("dve", lambda e, pl=pl, cur=cur, gi=gi, w=w, xp=xp: e.scalar_tensor_tensor(
                    pl[:, 2 * gi:2 * gi + 2, :], cur[:, :, 16:528], 1.0 / w, xp[:, 2 * gi:2 * gi + 2, 16:528], ALU.mult, ALU.subtract),
                    r=[rcur, rxp], w=[rpl])
                for mo in range(2):
                    ps, rps = pm.next()
                    for ki in range(2):
                        S.op("pe", lambda e, ps=ps, gi=gi, ki=ki, mo=mo, pl=pl: e.matmul(
                            ps[:], wpg[:, gi, ki, mo * 128:(mo + 1) * 128], pl[:, 2 * gi + ki, :], start=(ki == 0), stop=(ki == 1)),
                            r=[rwpg, rpl], w=[rps])
                    S.op("act", lambda e, mx=mx, ps=ps, c=2 * gi + mo: e.activation(mx[:, c, :], ps[:], AF.Copy, scale=psc[:, c:c + 1]),
                         r=[rps, rpsc], w=[rmx])
            for m in range(16):
                g2, rg2 = gr.next()
                src = bass.AP(T["gS"].tensor, T["gS"].offset + m * 128 * NOWN + t0, [[NOWN, 128], [2048 * NOWN, 2], [1, 512]])
                S.dma("sp", g2[:], src, w=[rg2])
                psa, rpsa = pa.next()
                psb, rpsb = pb.next()
                for k in range(4):
                    S.op("pe", lambda e, psa=psa, k=k, m=m, at=at: e.matmul(
                        psa[:], wba[:, k, m * 128:(m + 1) * 128], at[:, k, :], start=(k == 0), stop=(k == 3)), r=[rwba, rat], w=[rpsa])
                for k in range(8):
                    S.op("pe", lambda e, psb=psb, k=k, m=m, mx=mx: e.matmul(
                        psb[:], wbp[:, k, m * 128:(m + 1) * 128], mx[:, k, :], start=(k == 0), stop=(k == 7)), r=[rwbp, rmx], w=[rpsb])
                t, rt = tu.next()
                S.op("dve", lambda e, t=t, psa=psa, g2=g2: e.tensor_tensor(t[:, 0, :], psa[:], g2[:, 0, :], ALU.mult), r=[rpsa, rg2], w=[rt])
                S.op("dve", lambda e, t=t, psb=psb, g2=g2: e.tensor_tensor(t[:, 1, :], psb[:], g2[:, 1, :], ALU.mult), r=[rpsb, rg2, rt], w=[rt])
                S.op("pool", lambda e, t=t, me=me, m=m: e.tensor_tensor(me[:, m, :], t[:, 0, :], t[:, 1, :], ALU.add), r=[rt], w=[rme])
            S.dma("sp", T["mergedS"].rearrange("(c p) t -> p c t", p=128)[:, :, t0:t0 + 512], me[:], r=[rme])
        S.flush(final_wait=True)


def layer_norm_T(nc, S, C, h, rh, ncols, gam, bet, rgb, rings, out_writer):
    hb_r, hs_r, ps_mean, ps_msq, st_r, tmp_r = rings
    pmean, rpm = ps_mean.next()
    pmsq, rpq = ps_msq.next()
    for m in range(16):
        hb, rhb = hb_r.next()
        hs, rhs = hs_r.next()
        S.op("dve", lambda e, hb=hb, m=m: e.tensor_copy(hb[:, :ncols], h[:, m, :]), r=[rh], w=[rhb])
        S.op("act", lambda e, hs=hs, m=m: e.activation(hs[:, :ncols], h[:, m, :], AF.Square), r=[rh], w=[rhs])
        S.op("pe", lambda e, hb=hb, m=m: e.matmul(pmean[:, :ncols], C["onesd"][:], hb[:, :ncols], start=(m == 0), stop=(m == 15)),
             r=[rhb, C["r_onesd"]], w=[rpm])
        S.op("pe", lambda e, hs=hs, m=m: e.matmul(pmsq[:, :ncols], C["onesd"][:], hs[:, :ncols], start=(m == 0), stop=(m == 15)),
             r=[rhs, C["r_onesd"]], w=[rpq])
    st, rst = st_r.next()
    S.op("dve", lambda e: e.tensor_copy(st[:, 0, :ncols], pmean[:, :ncols]), r=[rpm], w=[rst])
    S.op("dve", lambda e: e.tensor_tensor(st[:, 2, :ncols], st[:, 0, :ncols], st[:, 0, :ncols], ALU.mult), r=[rst], w=[rst])
    S.op("dve", lambda e: e.tensor_tensor(st[:, 2, :ncols], pmsq[:, :ncols], st[:, 2, :ncols], ALU.subtract), r=[rst, rpq], w=[rst])
    S.op("dve", lambda e: e.tensor_scalar(st[:, 2, :ncols], st[:, 2, :ncols], 1e-5, None, ALU.add), r=[rst], w=[rst])
    S.op("act", lambda e: e.activation(st[:, 2, :ncols], st[:, 2, :ncols], AF.Sqrt), r=[rst], w=[rst])
    S.op("dve", lambda e: e.reciprocal(st[:, 1, :ncols], st[:, 2, :ncols]), r=[rst], w=[rst])
    for m in range(16):
        tp, rtp = tmp_r.next()
        S.op("dve", lambda e, tp=tp, m=m: e.tensor_tensor(tp[:, :ncols], h[:, m, :], st[:, 0, :ncols], ALU.subtract), r=[rh, rst], w=[rtp])
        S.op("pool", lambda e, tp=tp: e.tensor_tensor(tp[:, :ncols], tp[:, :ncols], st[:, 1, :ncols], ALU.mult), r=[rtp, rst], w=[rtp])
        S.op("act", lambda e, tp=tp, m=m: e.activation(h[:, m, :], tp[:, :ncols], AF.Identity, bias=bet[:, m:m + 1], scale=gam[:, m:m + 1]),
             r=[rtp, rgb], w=[rh])
        out_writer(m)


def phase_C2(nc, S, C, T, W, cfg):
    with ExitStack() as es:
        wo, rwo = sbt(S, es, nc, "D_wo", [128, 16, 2048], BF16)
        gb, rgb = sbt(S, es, nc, "D_gb", [128, 2, 16], F32)
        cast_load(S, wo[:], W["w_out"].rearrange("(k p) n -> p k n", p=128), rwo)
        S.dma("sp", gb[:, 0, :], W["ln1_g"], w=[rgb])
        S.dma("sp", gb[:, 1, :], W["ln1_b"], w=[rgb])
        mer = Ring(S, es, nc, "D_me", [128, 16, 512], BF16, 2)
        hr = Ring(S, es, nc, "D_h", [128, 16, 512], F32, 1)
        xbr = Ring(S, es, nc, "D_xb", [128, 16, 512], BF16, 1)
        pp = Ring(S, es, nc, "D_pp", [128, 512], F32, 3, psum=True)
        rings = (Ring(S, es, nc, "D_hb", [128, 512], BF16, 3), Ring(S, es, nc, "D_hs", [128, 512], BF16, 3),
                 Ring(S, es, nc, "D_pm", [128, 512], F32, 1, psum=True), Ring(S, es, nc, "D_pq", [128, 512], F32, 1, psum=True),
                 Ring(S, es, nc, "D_st", [128, 3, 512], F32, 1), Ring(S, es, nc, "D_tp", [128, 512], F32, 3))
        for tg in range(cfg.NOWN // 512):
            t0 = tg * 512
            me, rme = mer.next()
            h, rh = hr.next()
            xb, rxb = xbr.next()
            S.dma("sp", me[:], T["mergedS"].rearrange("(c p) t -> p c t", p=128)[:, :, t0:t0 + 512], w=[rme])
            S.dma("sp", h[:], T["xT"].rearrange("(c p) t -> p c t", p=128)[:, :, 2048 + t0:2048 + t0 + 512], w=[rh])
            for m in range(16):
                ps, rps = pp.next()
                for k in range(16):
                    S.op("pe", lambda e, ps=ps, k=k, m=m, me=me: e.matmul(
                        ps[:], wo[:, k, m * 128:(m + 1) * 128], me[:, k, :], start=(k == 0), stop=(k == 15)), r=[rwo, rme], w=[rps])
                S.op("dve", lambda e, ps=ps, m=m, h=h: e.scalar_tensor_tensor(
                    h[:, m, :], h[:, m, :], ALPHA, ps[:], ALU.mult, ALU.add), r=[rps, rh], w=[rh])

            def ow(m, h=h, xb=xb, rh=rh, rxb=rxb):
                S.op("act", lambda e: e.activation(xb[:, m, :], h[:, m, :], AF.Copy), r=[rh], w=[rxb], group=("xb", id(xb)))
            layer_norm_T(nc, S, C, h, rh, 512, gb[:, 0, :], gb[:, 1, :], rgb, rings, ow)
            S.dma("sp", T["x1S"].rearrange("(c p) t -> p c t", p=128)[:, :, t0:t0 + 512], h[:], r=[rh])
            S.dma("sp", T["x1bS"].rearrange("(c p) t -> p c t", p=128)[:, :, t0:t0 + 512], xb[:], r=[rxb])
        S.flush(final_wait=True)


def phase_D1(nc, S, C, T, W, cfg):
    with ExitStack() as es:
        wq, rwq = sbt(S, es, nc, "Q_wq", [128, 16, 2048], BF16)
        sk, rsk = sbt(S, es, nc, "Q_sk", [128, 16, 128], BF16)
        cast_load(S, wq[:], W["w_peer_q"].rearrange("(k p) n -> p k n", p=128), rwq)
        S.dma("pool", sk[:], W["skT"].rearrange("m d k -> d m k"), w=[rsk])
        xbr = Ring(S, es, nc, "Q_xb", [128, 16, 512], BF16, 2)
        qar = Ring(S, es, nc, "Q_qa", [128, 16, 512], BF16, 1)
        pp = Ring(S, es, nc, "Q_pp", [128, 512], F32, 3, psum=True)
        scp = Ring(S, es, nc, "Q_scp", [128, 2048], F32, 1, psum=True)
        scs = Ring(S, es, nc, "Q_scs", [128, 2048], F32, 2)
        fl = 0
        for tg in range(cfg.NOWN // 512):
            t0 = tg * 512
            xb, rxb = xbr.next()
            qa, rqa = qar.next()
            S.dma("sp", xb[:], T["x1bS"].rearrange("(c p) t -> p c t", p=128)[:, :, t0:t0 + 512], w=[rxb])
            for m in range(16):
                ps, rps = pp.next()
                for k in range(16):
                    S.op("pe", lambda e, ps=ps, k=k, m=m, xb=xb: e.matmul(
                        ps[:], wq[:, k, m * 128:(m + 1) * 128], xb[:, k, :], start=(k == 0), stop=(k == 15)), r=[rwq, rxb], w=[rps])
                fl ^= 1
                if fl:
                    S.op("act", lambda e, ps=ps, m=m, qa=qa: e.activation(qa[:, m, :], ps[:], AF.Copy), r=[rps], w=[rqa])
                else:
                    S.op("dve", lambda e, ps=ps, m=m, qa=qa: e.tensor_copy(qa[:, m, :], ps[:]), r=[rps], w=[rqa])
            for tt in range(4):
                sp_, rsp = scp.next()
                for m in range(16):
                    S.op("pe", lambda e, sp_=sp_, m=m, tt=tt, qa=qa: e.matmul(
                        sp_[:, m * 128:(m + 1) * 128], qa[:, m, tt * 128:(tt + 1) * 128], sk[:, m, :], start=True, stop=True),
                        r=[rqa, rsk], w=[rsp])
                ss, rss = scs.next()
                S.op("act", lambda e, ss=ss, sp_=sp_: e.activation(ss[:], sp_[:], AF.Copy), r=[rsp], w=[rss])
                S.dma("sp", T["scS"][t0 + tt * 128:t0 + tt * 128 + 128, :], ss[:], r=[rss])
        S.flush(final_wait=True)


def top16_multi(S, items, rsrc, rtmp, rout, rmid):
    gid = ("t16", id(items))
    for (src, tmp, vals, idx) in items:
        S.op("dve", lambda e, src=src, vals=vals: e.max(vals[:, 0:8], src), r=[rsrc], w=[rmid], group=gid + (0,))
    for (src, tmp, vals, idx) in items:
        S.op("dve", lambda e, src=src, vals=vals, idx=idx: e.max_index(idx[:, 0:8], vals[:, 0:8], src), r=[rsrc, rmid], w=[rout], group=gid + (1,))
    for (src, tmp, vals, idx) in items:
        S.op("dve", lambda e, src=src, tmp=tmp, vals=vals: e.match_replace(tmp, vals[:, 0:8], src, NEG), r=[rsrc, rmid], w=[rtmp], group=gid + (2,))
    for (src, tmp, vals, idx) in items:
        S.op("dve", lambda e, tmp=tmp, vals=vals: e.max(vals[:, 8:16], tmp), r=[rtmp], w=[rmid], group=gid + (3,))
    for (src, tmp, vals, idx) in items:
        S.op("dve", lambda e, tmp=tmp, vals=vals, idx=idx: e.max_index(idx[:, 8:16], vals[:, 8:16], tmp), r=[rtmp, rmid], w=[rout], group=gid + (4,))


def phase_D2(nc, S, C, T, cfg):
    with ExitStack() as es:
        scr = Ring(S, es, nc, "R_sc", [128, 16, 128], F32, 2)
        s2r = Ring(S, es, nc, "R_s2", [128, 16, 128], F32, 1)
        svr = Ring(S, es, nc, "R_sv", [128, 256], F32, 2)
        siur = Ring(S, es, nc, "R_siu", [128, 256], U32, 2)
        sifr = Ring(S, es, nc, "R_sif", [128, 256], F32, 2)
        cdr = Ring(S, es, nc, "R_cd", [128, 8, 256], F32, 2)
        cd2r = Ring(S, es, nc, "R_cd2", [128, 8, 256], F32, 1)
        cvr = Ring(S, es, nc, "R_cv", [128, 128], F32, 2)
        ciur = Ring(S, es, nc, "R_ciu", [128, 128], U32, 2)
        exr = Ring(S, es, nc, "R_ex", [128, 128], F32, 2)
        smr = Ring(S, es, nc, "R_sm", [128, 8], F32, 2)
        hlur = Ring(S, es, nc, "R_hlu", [128, 2, 128], U32, 2)
        hlfr = Ring(S, es, nc, "R_hlf", [128, 2, 128], F32, 2)
        eqr = Ring(S, es, nc, "R_eq", [128, 2048], F32, 2)
        prr = Ring(S, es, nc, "R_pr", [128, 2048], F32, 2)
        trr = Ring(S, es, nc, "R_tr", [128, 3, 128], F32, 2)
        t3r = Ring(S, es, nc, "R_t3", [128, 3, 128], F32, 2)
        ptr = Ring(S, es, nc, "R_pt", [128, 512], F32, 2, psum=True)
        a1r = Ring(S, es, nc, "R_a1", [128, 128], BF16, 8)
        a2r = Ring(S, es, nc, "R_a2", [128, 128], BF16, 8)
        gpr = Ring(S, es, nc, "R_gp", [128, 512], F32, 3, psum=True)
        gtr = Ring(S, es, nc, "R_gt", [128, 128, 128], BF16, 2)
        iota = C["iota"]
        for tl in range(cfg.NOWN // 128):
            t0 = tl * 128
            sc, rsc = scr.next()
            s2, rs2 = s2r.next()
            sv, rsv = svr.next()
            siu, rsiu = siur.next()
            sif, rsif = sifr.next()
            S.dma("sp", sc[:], T["scS"][t0:t0 + 128, :].rearrange("t (m k) -> t m k", k=128), w=[rsc])
            rmid = S.res()
            top16_multi(S, [(sc[:, j, :], s2[:, j, :], sv[:, j * 16:(j + 1) * 16], siu[:, j * 16:(j + 1) * 16]) for j in range(16)],
                        rsc, rs2, rsv, rmid)
            S.op("dve", lambda e, sif=sif, siu=siu: e.tensor_copy(sif[:], siu[:]), r=[rsv, rmid], w=[rsif])
            cd, rcd = cdr.next()
            cd2, rcd2 = cd2r.next()
            S.op("dve", lambda e, cd=cd, sv=sv: e.tensor_tensor(
                bass.AP(cd, 0, [[2048, 128], [256, 8], [16, 16], [1, 16]]),
                bass.AP(sv, 0, [[256, 128], [32, 8], [1, 16], [0, 16]]),
                bass.AP(sv, 16, [[256, 128], [32, 8], [0, 16], [1, 16]]), ALU.add), r=[rsv, rmid], w=[rcd])
            cv, rcv = cvr.next()
            ciu, rciu = ciur.next()
            rmid2 = S.res()
            top16_multi(S, [(cd[:, h, :], cd2[:, h, :], cv[:, h * 16:(h + 1) * 16], ciu[:, h * 16:(h + 1) * 16]) for h in range(8)],
                        rcd, rcd2, rcv, rmid2)
            ex, rex = exr.next()
            sm, rsm = smr.next()
            tr, rtr = trr.next()
            S.op("dve", lambda e, ex=ex, cv=cv: e.tensor_tensor(
                bass.AP(ex, 0, [[128, 128], [16, 8], [1, 16]]), bass.AP(cv, 0, [[128, 128], [16, 8], [1, 16]]),
                bass.AP(cv, 0, [[128, 128], [16, 8], [0, 16]]), ALU.subtract), r=[rcv, rmid2], w=[rex])
            S.op("act", lambda e, ex=ex: e.activation(ex[:], ex[:], AF.Exp), r=[rex], w=[rex])
            S.op("dve", lambda e, ex=ex, sm=sm: e.tensor_reduce(sm[:], bass.AP(ex, 0, [[128, 128], [16, 8], [1, 16]]), AX.X, ALU.add),
                 r=[rex], w=[rsm])
            S.op("dve", lambda e, sm=sm: e.reciprocal(sm[:], sm[:]), r=[rsm], w=[rsm])
            S.op("dve", lambda e, tr=tr, ex=ex, sm=sm: e.tensor_tensor(
                bass.AP(tr, 256, [[384, 128], [16, 8], [1, 16]]), bass.AP(ex, 0, [[128, 128], [16, 8], [1, 16]]),
                bass.AP(sm, 0, [[8, 128], [1, 8], [0, 16]]), ALU.mult), r=[rex, rsm], w=[rtr])
            hlu, rhlu = hlur.next()
            hlf, rhlf = hlfr.next()
            S.op("dve", lambda e, hlu=hlu, ciu=ciu: e.tensor_single_scalar(hlu[:, 0, :], ciu[:], 4, ALU.logical_shift_right), r=[rcv], w=[rhlu])
            S.op("dve", lambda e, hlu=hlu, ciu=ciu: e.tensor_single_scalar(hlu[:, 1, :], ciu[:], 15, ALU.bitwise_and), r=[rcv], w=[rhlu])
            S.op("dve", lambda e, hlf=hlf, hlu=hlu: e.tensor_copy(hlf[:], hlu[:]), r=[rhlu], w=[rhlf])
            for half in range(2):
                eq, req = eqr.next()
                pr, rpr = prr.next()
                S.op("dve", lambda e, eq=eq, hlf=hlf, half=half: e.tensor_tensor(
                    bass.AP(eq, 0, [[2048, 128], [16, 128], [1, 16]]),
                    bass.AP(iota, 0, [[128, 128], [0, 128], [1, 16]]),
                    bass.AP(hlf, half * 128, [[256, 128], [1, 128], [0, 16]]), ALU.is_equal), r=[rhlf, C["r_iota"]], w=[req])
                S.op("pool", lambda e, pr=pr, eq=eq, sif=sif, half=half: e.tensor_tensor(
                    bass.AP(pr, 0, [[2048, 128], [256, 8], [16, 16], [1, 16]]),
                    bass.AP(eq, 0, [[2048, 128], [256, 8], [16, 16], [1, 16]]),
                    bass.AP(sif, half * 16, [[256, 128], [32, 8], [0, 16], [1, 16]]), ALU.mult), r=[req, rsif], w=[rpr])
                S.op("dve", lambda e, tr=tr, pr=pr, half=half: e.tensor_reduce(
                    tr[:, half, :], bass.AP(pr, 0, [[2048, 128], [16, 128], [1, 16]]), AX.X, ALU.add), r=[rpr], w=[rtr])
            pt, rpt = ptr.next()
            t3, rt3 = t3r.next()
            for i in range(3):
                S.op("pe", lambda e, pt=pt, tr=tr, i=i: e.transpose(pt[:, i * 128:(i + 1) * 128], tr[:, i, :], C["ident"][:]),
                     r=[rtr, C["r_ident"]], w=[rpt])
            S.op("act", lambda e, t3=t3, pt=pt: e.activation(t3[:].rearrange("p a b -> p (a b)"), pt[:, 0:384], AF.Copy), r=[rpt], w=[rt3])
            gt, rgt = gtr.next()
            for tq in range(32):
                gp, rgp = gpr.next()
                for u in range(4):
                    ti = tq * 4 + u
                    a1, ra1 = a1r.next()
                    a2, ra2 = a2r.next()
                    S.op("dve", lambda e, a1=a1, t3=t3, ti=ti: e.tensor_scalar(
                        a1[:], iota[:], t3[:, 1, ti:ti + 1], t3[:, 2, ti:ti + 1], ALU.is_equal, ALU.mult), r=[rt3, C["r_iota"]], w=[ra1])
                    S.op("dve", lambda e, a2=a2, t3=t3, ti=ti: e.tensor_scalar(
                        a2[:], iota[:], t3[:, 0, ti:ti + 1], None, ALU.is_equal), r=[rt3, C["r_iota"]], w=[ra2])
                    S.op("pe", lambda e, gp=gp, a1=a1, a2=a2, u=u: e.matmul(gp[:, u * 128:(u + 1) * 128], a1[:], a2[:], start=True, stop=True),
                         r=[ra1, ra2], w=[rgp])
                S.op("act", lambda e, gt=gt, gp=gp, tq=tq: e.activation(
                    bass.AP(gt, tq * 4, [[16384, 128], [128, 128], [1, 4]]),
                    bass.AP(gp, 0, [[512, 128], [1, 128], [128, 4]]), AF.Copy), r=[rgp], w=[rgt])
            S.dma("sp", T["GS"][:, :, t0:t0 + 128], gt[:], r=[rgt])
        S.flush(final_wait=True)


def phase_E(nc, S, C, T, W, cfg):
    for st in range(cfg.NOWN // 1024):
        T0 = st * 1024
        with ExitStack() as es:
            xb, rxb = sbt(S, es, nc, "E_xb", [128, 16, 1024], BF16)
            acc, racc = sbt(S, es, nc, "E_acc", [128, 16, 1024], F32)
            S.dma("sp", xb[:], T["x1bS"].rearrange("(c p) t -> p c t", p=128)[:, :, T0:T0 + 1024], w=[rxb])
            with ExitStack() as es2:
                ur = Ring(S, es2, nc, "E_u", [128, 16, 256], BF16, 3)
                vr = Ring(S, es2, nc, "E_v", [128, 2, 2048], BF16, 3)
                gr = Ring(S, es2, nc, "E_g", [128, 2, 1024], BF16, 3)
                atr = Ring(S, es2, nc, "E_at", [128, 2, 1024], BF16, 2)
                glr = Ring(S, es2, nc, "E_gl", [128, 512], BF16, 3)
                hp = Ring(S, es2, nc, "E_hp", [128, 512], F32, 4, psum=True)
                vp = Ring(S, es2, nc, "E_vp", [128, 512], F32, 4, psum=True)
                tiles = {}

                def load(s):
                    e0 = s * 256
                    u, ru = ur.next()
                    v, rv = vr.next()
                    g, rg = gr.next()
                    at, rat = atr.next()
                    S.dma("pool", u[:], W["peer_uT"].rearrange("(k p) e -> p k e", p=128)[:, :, e0:e0 + 256], w=[ru])
                    cast_load(S, v[:], W["peer_v"][e0:e0 + 256, :].rearrange("(c p) f -> p c f", p=128), rv)
                    S.dma("sp", g[:], T["GS"][:, 2 * s:2 * s + 2, T0:T0 + 1024], w=[rg])
                    tiles[s] = (u, ru, v, rv, g, rg, at, rat)

                def u_steps(s):
                    u, ru, v, rv, g, rg, at, rat = tiles[s]
                    for c in range(2):
                        pss = [hp.next(), hp.next()]
                        for k in range(16):
                            for half in range(2):
                                ps, rps = pss[half]
                                S.op("pe", lambda e, ps=ps, k=k, c=c, half=half, u=u: e.matmul(
                                    ps[:], u[:, k, c * 128:(c + 1) * 128], xb[:, k, half * 512:(half + 1) * 512],
                                    start=(k == 0), stop=(k == 15)), r=[ru, rxb], w=[rps])
                            if k % 2 == 1:
                                yield
                        for half in range(2):
                            ps, rps = pss[half]
                            gl, rgl = glr.next()
                            S.op("act", lambda e, gl=gl, ps=ps: e.activation(gl[:], ps[:], AF.Gelu), r=[rps], w=[rgl])
                            S.op("pool", lambda e, at=at, gl=gl, g=g, c=c, half=half: e.tensor_tensor(
                                at[:, c, half * 512:(half + 1) * 512], gl[:], g[:, c, half * 512:(half + 1) * 512], ALU.mult),
                                r=[rgl, rg], w=[rat])

                def v_steps(s):
                    u, ru, v, rv, g, rg, at, rat = tiles[s]
                    for m in range(16):
                        pss = [vp.next(), vp.next()]
                        for c in range(2):
                            for half in range(2):
                                ps, rps = pss[half]
                                S.op("pe", lambda e, ps=ps, c=c, m=m, half=half, v=v, at=at: e.matmul(
                                    ps[:], v[:, c, m * 128:(m + 1) * 128], at[:, c, half * 512:(half + 1) * 512],
                                    start=(c == 0), stop=(c == 1)), r=[rv, rat], w=[rps])
                        for half in range(2):
                            ps, rps = pss[half]
                            if s == 0:
                                S.op("dve", lambda e, ps=ps, m=m, half=half: e.tensor_copy(acc[:, m, half * 512:(half + 1) * 512], ps[:]),
                                     r=[rps], w=[racc], group="accw")
                            else:
                                S.op("dve", lambda e, ps=ps, m=m, half=half: e.tensor_tensor(
                                    acc[:, m, half * 512:(half + 1) * 512], ps[:], acc[:, m, half * 512:(half + 1) * 512], ALU.add),
                                    r=[rps], w=[racc], group="accw")
                        yield

                NS = 64
                load(0)
                load(1)
                for _ in u_steps(0):
                    pass
                for s in range(1, NS + 1):
                    if s + 1 < NS:
                        load(s + 1)
                    if s < NS:
                        ug = u_steps(s)
                    else:
                        ug = iter(())
                    vg = v_steps(s - 1)
                    for _ in vg:
                        next(ug, None)
                    for _ in ug:
                        pass
                    del tiles[s - 1]
                S.flush(final_wait=True)
            with ExitStack() as es3:
                gb, rgb = sbt(S, es3, nc, "E_gb", [128, 2, 16], F32)
                S.dma("sp", gb[:, 0, :], W["ln2_g"], w=[rgb])
                S.dma("sp", gb[:, 1, :], W["ln2_b"], w=[rgb])
                x1r = Ring(S, es3, nc, "E_x1", [128, 1024], F32, 2)
                rings = (Ring(S, es3, nc, "E_hb", [128, 512], BF16, 3), Ring(S, es3, nc, "E_hs", [128, 512], BF16, 3),
                         Ring(S, es3, nc, "E_pm", [128, 512], F32, 1, psum=True), Ring(S, es3, nc, "E_pq", [128, 512], F32, 1, psum=True),
                         Ring(S, es3, nc, "E_st", [128, 3, 512], F32, 1), Ring(S, es3, nc, "E_tp", [128, 512], F32, 3))
                for m in range(16):
                    x1, rx1 = x1r.next()
                    S.dma("sp", x1[:], T["x1S"][m * 128:(m + 1) * 128, T0:T0 + 1024], w=[rx1])
                    S.op("dve", lambda e, x1=x1, m=m: e.scalar_tensor_tensor(
                        acc[:, m, :], x1[:], ALPHA, acc[:, m, :], ALU.mult, ALU.add), r=[rx1, racc], w=[racc])
                for half in range(2):
                    hv = acc[:, :, half * 512:(half + 1) * 512]
                    layer_norm_T(nc, S, C, hv, racc, 512, gb[:, 0, :], gb[:, 1, :], rgb, rings, lambda m: None)
                S.dma("sp", T["outT"].rearrange("(c p) t -> p c t", p=128)[:, :, T0:T0 + 1024], acc[:], r=[racc])
                S.flush(final_wait=True)
PHASES = ("A", "B", "C1", "C2", "D1", "D2", "E")

class Cfg:
    def __init__(self, layer, nown, coff, mask_blk_col, corr_tg):
        self.layer = layer
        self.NOWN = nown
        self.NALL = nown + 2048
        self.coff = coff
        self.mask_blk_col = mask_blk_col
        self.corr_tg = corr_tg
        self.suffix = f"_L{layer}"

    def mask_qb(self, r):
        return (self.mask_blk_col // r) // 128


def scratch_shapes(cfg):
    n, a = cfg.NOWN, cfg.NALL
    return {
        "qS": ([1536, n], BF16), "kS": ([1536, a], BF16), "vS": ([a, 1536], BF16),
        "xpS": ([1024, 16 + n], F32), "gS": ([4096, n], BF16), "attnS": ([512, n], BF16),
        "mergedS": ([2048, n], BF16), "x1S": ([2048, n], F32), "x1bS": ([2048, n], BF16),
        "scS": ([n, 2048], F32), "GS": ([128, 128, n], BF16),
    }


INPUTS_T = {
    "cosT": [128, 8192], "sinT": [128, 8192], "ident": [128, 128], "iota": [128, 128],
    "pswap": [128, 128], "masks": [128, 2, 256], "flag": [128, 1], "corr": [128, 4, 16],
}
INPUTS_W = {
    "w_in": [2048, 9728], "b_gate": [128, 32], "w_branch_attn": [512, 2048], "w_branch_pool": [1024, 2048],
    "w_pool_group": [4, 256, 256], "pool_scale": [128, 8], "w_out": [2048, 2048], "ln1_g": [128, 16], "ln1_b": [128, 16],
    "w_peer_q": [2048, 2048], "skT": [16, 128, 128], "peer_uT": [2048, 16384], "peer_v": [16384, 2048],
    "ln2_g": [128, 16], "ln2_b": [128, 16],
}


class LazyT(dict):
    def __init__(self, nc, cfg, shared, used_inputs, x_in=None, x_out=None):
        super().__init__()
        self.nc = nc
        self.cfg = cfg
        self.shared = shared
        self.used_inputs = used_inputs
        self.sc = scratch_shapes(cfg)
        if x_in is not None:
            self["xT"] = x_in
        if x_out is not None:
            self["outT"] = x_out

    def __missing__(self, k):
        nc = self.nc
        sfx = self.cfg.suffix
        if k in self.sc:
            shape, dt = self.sc[k]
            v = nc.dram_tensor(k + sfx, shape, dt, kind="Internal").ap()
        elif k in INPUTS_T:
            if k not in self.shared:
                self.shared[k] = nc.dram_tensor(k, INPUTS_T[k], F32, kind="ExternalInput").ap()
                self.used_inputs.append(k)
            v = self.shared[k]
        elif k in INPUTS_W:
            v = nc.dram_tensor(k + sfx, INPUTS_W[k], F32, kind="ExternalInput").ap()
            self.used_inputs.append(k + sfx)
        else:
            raise KeyError(k)
        self[k] = v
        return v


def emit_layer(nc, S, C, T, cfg):
    phase_A(nc, S, C, T, T, cfg)
    phase_B(nc, S, C, T, cfg)
    phase_C1(nc, S, C, T, T, cfg)
    phase_C2(nc, S, C, T, T, cfg)
    phase_D1(nc, S, C, T, T, cfg)
    phase_D2(nc, S, C, T, cfg)
    phase_E(nc, S, C, T, T, cfg)


def build_program():
    nc = bass.Bass("TRN2", target_bir_lowering=False)
    used = ["xT"]
    shared = {}
    x0 = nc.dram_tensor("xT", [2048, 8192], F32, kind="ExternalInput").ap()
    xmid = nc.dram_tensor("xmid", [2048, 6144], F32, kind="Internal").ap()
    out = nc.dram_tensor("outT", [2048, 4096], F32, kind="ExternalOutput").ap()
    cfg0 = Cfg(0, 6144, 0, 2048, 4)
    cfg1 = Cfg(1, 4096, 2048, 0, 0)
    with ExitStack() as es:
        S = Sched(nc, es)
        T0 = LazyT(nc, cfg0, shared, used, x_in=x0, x_out=xmid)
        T1 = LazyT(nc, cfg1, shared, used, x_in=xmid, x_out=out)
        C = load_consts(nc, S, es, T0)
        S.flush(final_wait=True)
        emit_layer(nc, S, C, T0, cfg0)
        emit_layer(nc, S, C, T1, cfg1)
    return nc, used


def rope_tables_np(pos):
    inv = (np.float32(10000.0) ** (-np.arange(0, 128, 2, dtype=np.float32) / np.float32(128))).astype(np.float32)
    ang = pos.astype(np.float32)[:, None] * inv[None, :]
    ang = np.concatenate([ang, ang], axis=-1)
    cos = np.cos(ang).astype(np.float32).T
    sin = np.sin(ang).astype(np.float32).T
    sgn = np.where(np.arange(128) < 64, -1.0, 1.0).astype(np.float32)[:, None]
    return np.ascontiguousarray(cos), np.ascontiguousarray(sin * sgn)


def const_inputs(core):
    ci = core % 4
    flag = 1.0 if ci > 0 else 0.0
    pos = np.maximum(ci * 4096 - 4096 + np.arange(8192), 0)
    cosT, sinT = rope_tables_np(pos)
    ident = np.eye(128, dtype=np.float32)
    iota = np.tile(np.arange(128, dtype=np.float32)[None, :], (128, 1))
    pswap = np.zeros((128, 128), np.float32)
    pswap[np.arange(128), (np.arange(128) + 64) % 128] = 1.0
    kk = np.arange(128)[:, None]
    qq = np.arange(128)[None, :]
    prev = (kk >= qq).astype(np.float32)
    cur = (kk <= qq).astype(np.float32)
    masks = np.zeros((128, 2, 256), np.float32)
    masks[:, 0, :128] = prev
    masks[:, 0, 128:] = cur
    masks[:, 1, :128] = prev * flag
    masks[:, 1, 128:] = cur
    corr = np.ones((128, 4, 16), np.float32)
    if ci == 0:
        t = np.arange(16)
        for gi, w in enumerate((2, 4, 8, 16)):
            corr[:, gi, :] = (w / np.minimum(t + 1, w)).astype(np.float32)[None, :]
    return {"cosT": cosT, "sinT": sinT, "ident": ident, "iota": iota, "pswap": pswap, "masks": masks,
            "flag": np.full((128, 1), flag, np.float32), "corr": corr}


def colvec(v, n):
    return np.ascontiguousarray(np.asarray(v, np.float32).reshape(n, 128).T)


def layer_weights(inp, l):
    return {
        "w_in": np.ascontiguousarray(inp["w_in"][l]),
        "b_gate": colvec(inp["b_gate"][l].reshape(-1), 32),
        "w_branch_attn": np.ascontiguousarray(inp["w_branch_attn"][l]),
        "w_branch_pool": np.ascontiguousarray(inp["w_branch_pool"][l]),
        "w_pool_group": np.ascontiguousarray(inp["w_pool_group"][l]),
        "pool_scale": colvec(inp["pool_scale"][l], 8),
        "w_out": np.ascontiguousarray(inp["w_out"][l]),
        "ln1_g": colvec(inp["ln1_g"][l], 16), "ln1_b": colvec(inp["ln1_b"][l], 16),
        "w_peer_q": np.ascontiguousarray(inp["w_peer_q"][l]),
        "skT": np.ascontiguousarray(inp["peer_subkeys"][l].reshape(16, 128, 128).transpose(0, 2, 1)),
        "peer_uT": np.ascontiguousarray(inp["peer_u"][l].T),
        "peer_v": np.ascontiguousarray(inp["peer_v"][l]),
        "ln2_g": colvec(inp["ln2_g"][l], 16), "ln2_b": colvec(inp["ln2_b"][l], 16),
    }


def make_xT(xfull, core):
    b, ci = core // 4, core % 4
    s0 = ci * 4096
    xT = np.zeros((2048, 8192), np.float32)
    xT[:, 4096:] = xfull[b, s0:s0 + 4096].T
    if ci > 0:
        xT[:, :4096] = xfull[b, s0 - 4096:s0].T
    return xT


_PROG = []


def kernel(**inp):
    x = np.asarray(inp["x"], np.float32)
    if not _PROG:
        _PROG.append(build_program())
    nc, used = _PROG[0]
    wl = [layer_weights(inp, l) for l in range(2)]
    in_maps = []
    for c in range(8):
        m = dict(const_inputs(c))
        m["xT"] = make_xT(x, c)
        for l in range(2):
            for k, v in wl[l].items():
                m[f"{k}_L{l}"] = v
        in_maps.append({k: m[k] for k in used})
    res = run_bass_kernel_spmd(nc, in_maps, core_ids=list(range(8))).results
    out = np.empty_like(x)
    for c in range(8):
        b, ci = c // 4, c % 4
        out[b, ci * 4096:(ci + 1) * 4096] = res[c]["outT"].T
    return out
```
